# Optimizing a Trainium2 kernel written in Bass

```python
import jax, jax.numpy as jnp
from jax import lax
import numpy as np

D_MODEL = 2048
BATCH = 4
SEQ = 4096
DEPTH = 2

GRID_W = 64
CTX_LEN = 256
N_BRANCH = 4
BRANCH_W = D_MODEL // 4
RET_HEADS = 4
RET_DK = BRANCH_W // RET_HEADS
RET_CHUNK = 128
ROPE_PAIRS = RET_DK // 4
ROPE_BASE = 10000.0
FNET_GROUPS = 4
FNET_GW = BRANCH_W // FNET_GROUPS
SC_WIDTH = 3
CF_WIDTH = 31
D_FF = 4 * D_MODEL
IN_COLS = 10 * BRANCH_W + N_BRANCH * D_MODEL
EPS = 1e-6

kernel_name = "hybrid_retention_fnet_conv_dit_block"


def rms_norm(x, g):
    x32 = x.astype(jnp.float32)
    y = x32 * lax.rsqrt(jnp.mean(x32 * x32, axis=-1, keepdims=True) + EPS)
    return (y * g.astype(jnp.float32)).astype(x.dtype)


def layer_norm(x, g, b):
    x32 = x.astype(jnp.float32)
    mu = jnp.mean(x32, axis=-1, keepdims=True)
    var = jnp.mean(jnp.square(x32 - mu), axis=-1, keepdims=True)
    y = (x32 - mu) * lax.rsqrt(var + EPS)
    return (y * g.astype(jnp.float32) + b.astype(jnp.float32)).astype(x.dtype)


def depthwise_conv(u, w):
    return lax.conv_general_dilated(u, w[:, None, :].astype(u.dtype), window_strides=(1,), padding="SAME",
                                    dimension_numbers=("NWC", "WIO", "NWC"), feature_group_count=u.shape[-1])


def fourier_mix(u):
    b, l, _ = u.shape
    ug = u.astype(jnp.float32).reshape(b, l, FNET_GROUPS, FNET_GW)
    y = jnp.real(jnp.fft.fftn(ug, axes=(1, 3), norm="ortho"))
    return y.reshape(b, l, BRANCH_W).astype(u.dtype)


def _heads(t):
    b, l, _ = t.shape
    return t.astype(jnp.float32).reshape(b, l, RET_HEADS, RET_DK)


def rope2d(t, cos, sin):
    half = RET_DK // 2
    t1, t2 = t[..., :half], t[..., half:]
    cs, sn = cos[None, :, None, :], sin[None, :, None, :]
    return jnp.concatenate([t1 * cs - t2 * sn, t1 * sn + t2 * cs], axis=-1)


def retention_chunkwise(q, k, v, log_g, s0, strict):
    b, l, h, d = q.shape
    n = l // RET_CHUNK
    idx = jnp.arange(RET_CHUNK, dtype=jnp.float32)
    diff = idx[:, None] - idx[None, :]
    keep = (diff > 0) if strict else (diff >= 0)
    intra = jnp.where(keep[None], jnp.exp(log_g[:, None, None] * jnp.maximum(diff, 0.0)[None]), 0.0)
    q_dec = jnp.exp(log_g[:, None] * (idx + 1.0)[None])
    k_dec = jnp.exp(log_g[:, None] * (RET_CHUNK - 1.0 - idx)[None])
    chunk_dec = jnp.exp(log_g * RET_CHUNK)

    def to_chunks(t):
        return t.reshape(b, n, RET_CHUNK, h, d).transpose(1, 0, 3, 2, 4)

    def step(s, qkv):
        qc, kc, vc = qkv
        scores = jnp.einsum("bhqd,bhkd->bhqk", qc, kc) * intra
        y = jnp.einsum("bhqk,bhkv->bhqv", scores, vc) + jnp.einsum("bhqd,bhdv->bhqv", qc * q_dec[..., None], s)
        s = chunk_dec[:, None, None] * s + jnp.einsum("bhkd,bhkv->bhdv", kc * k_dec[..., None], vc)
        return s, y

    _, y = lax.scan(step, s0, (to_chunks(q), to_chunks(k), to_chunks(v)))
    return y.transpose(1, 0, 3, 2, 4).reshape(b, l, h, d)


def context_state(k, v, log_g):
    l = k.shape[1]
    w = jnp.exp(log_g[:, None] * (l - 1.0 - jnp.arange(l, dtype=jnp.float32))[None])
    return jnp.einsum("blhd,blhv,hl->bhdv", k, v, w)


def bidir_retention(q, k, v, lg_f, lg_b, s_f, s_b):
    y_f = retention_chunkwise(q, k, v, lg_f, s_f, False)
    y_b = retention_chunkwise(q[:, ::-1], k[:, ::-1], v[:, ::-1], lg_b, s_b, True)[:, ::-1]
    return y_f + y_b


def retention_out(y, gate):
    mu = jnp.mean(y, axis=-1, keepdims=True)
    var = jnp.mean(jnp.square(y - mu), axis=-1, keepdims=True)
    yn = ((y - mu) * lax.rsqrt(var + EPS)).reshape(y.shape[0], y.shape[1], BRANCH_W)
    return yn.astype(gate.dtype) * jax.nn.silu(gate)


def merge_branches(parts, y_ret, sc_conv, cf_conv, cf_ln, w_branch, w_out):
    u_f, sc_b, sc_c, sc_x, cf_a, cf_b, gates = parts[4:]
    y_f = fourier_mix(u_f)
    y_sc = sc_b * depthwise_conv(sc_c * sc_x, sc_conv)
    y_cf = jax.nn.silu(layer_norm(depthwise_conv(cf_a * jax.nn.sigmoid(cf_b), cf_conv), cf_ln[0], cf_ln[1]))
    ybr = jnp.stack([y_ret, y_f, y_sc, y_cf], axis=2)
    proj = jnp.einsum("blnw,nwd->blnd", ybr, w_branch)
    b, l, _ = gates.shape
    g = jax.nn.sigmoid(gates.reshape(b, l, N_BRANCH, D_MODEL))
    return jnp.sum(g * proj, axis=2) @ w_out


def sq_relu_mlp(h, w1, w2):
    return jnp.square(jax.nn.relu(h @ w1)) @ w2


def hybrid_layer(x, xc, mod, mod_c, w_in, norm_g, ret_decay, sc_conv, cf_conv, cf_ln, w_branch, w_out,
                 w_ff1, w_ff2, cos, sin, with_ctx):
    sh1, sc1, gt1, sh2, sc2, gt2 = jnp.split(mod[:, None, :], 6, axis=-1)
    csh1, csc1, cgt1, csh2, csc2, cgt2 = jnp.split(mod_c, 6, axis=-1)
    splits = [BRANCH_W * i for i in range(1, 11)]

    h = rms_norm(x, norm_g[0]) * (1.0 + sc1) + sh1
    hc = rms_norm(xc, norm_g[0]) * (1.0 + csc1) + csh1
    parts = jnp.split(h @ w_in, splits, axis=-1)
    if with_ctx:
        parts_c = jnp.split(hc @ w_in, splits, axis=-1)
    else:
        parts_c = jnp.split(hc @ w_in[:, :2 * BRANCH_W], [BRANCH_W], axis=-1)

    lg_f, lg_b = jax.nn.log_sigmoid(ret_decay.astype(jnp.float32))
    k_scale = RET_DK ** -0.5
    k_c = _heads(parts_c[0]) * k_scale
    v_c = _heads(parts_c[1])
    s_f = context_state(k_c, v_c, lg_f)
    s_b = context_state(k_c[:, ::-1], v_c[:, ::-1], lg_b)

    q = rope2d(_heads(parts[2]), cos, sin)
    k = rope2d(_heads(parts[0]), cos, sin) * k_scale
    v = _heads(parts[1])
    y_ret = retention_out(bidir_retention(q, k, v, lg_f, lg_b, s_f, s_b), parts[3])
    mix = merge_branches(parts, y_ret, sc_conv, cf_conv, cf_ln, w_branch, w_out)
    x = x + gt1 * rms_norm(mix, norm_g[1])
    h2 = rms_norm(x, norm_g[2]) * (1.0 + sc2) + sh2
    x = x + gt2 * rms_norm(sq_relu_mlp(h2, w_ff1, w_ff2), norm_g[3])

    if with_ctx:
        zeros = jnp.zeros_like(s_f)
        y_ret_c = retention_out(bidir_retention(_heads(parts_c[2]), k_c, v_c, lg_f, lg_b, zeros, zeros), parts_c[3])
        mix_c = merge_branches(parts_c, y_ret_c, sc_conv, cf_conv, cf_ln, w_branch, w_out)
        xc = xc + cgt1 * rms_norm(mix_c, norm_g[1])
        hc2 = rms_norm(xc, norm_g[2]) * (1.0 + csc2) + csh2
        xc = xc + cgt2 * rms_norm(sq_relu_mlp(hc2, w_ff1, w_ff2), norm_g[3])
    return x, xc


def setup_inputs(seed: int = 0) -> dict:
    key = jax.random.key(seed)
    ks = jax.random.split(key, 18)
    f32 = jnp.float32

    def nrm(k, shape, scale):
        return jax.random.normal(k, shape, f32) * scale

    x = nrm(ks[0], (BATCH, SEQ, D_MODEL), 1.0)
    c = nrm(ks[1], (BATCH, D_MODEL), 1.0)
    ctx = nrm(ks[2], (BATCH, CTX_LEN, D_MODEL), 1.0)
    c_ctx = nrm(ks[3], (D_MODEL,), 1.0)
    w_ada = nrm(ks[4], (DEPTH, D_MODEL, 6 * D_MODEL), 0.5 * D_MODEL ** -0.5)
    b_ada = nrm(ks[5], (DEPTH, 6 * D_MODEL), 0.02)
    norm_g = 1.0 + nrm(ks[6], (DEPTH, 4, D_MODEL), 0.02)
    w_in = nrm(ks[7], (DEPTH, D_MODEL, IN_COLS), D_MODEL ** -0.5)
    base = jnp.log(2.0 ** (5.0 + jnp.arange(RET_HEADS, dtype=f32)) - 1.0)
    ret_decay = base[None, None, :] + nrm(ks[8], (DEPTH, 2, RET_HEADS), 0.1)
    sc_conv = nrm(ks[9], (DEPTH, SC_WIDTH, BRANCH_W), SC_WIDTH ** -0.5)
    cf_conv = nrm(ks[10], (DEPTH, CF_WIDTH, BRANCH_W), CF_WIDTH ** -0.5)
    cf_ln = jnp.stack([1.0 + nrm(ks[11], (DEPTH, BRANCH_W), 0.02), nrm(ks[12], (DEPTH, BRANCH_W), 0.02)], axis=1)
    w_branch = nrm(ks[13], (DEPTH, N_BRANCH, BRANCH_W, D_MODEL), BRANCH_W ** -0.5)
    w_out = nrm(ks[14], (DEPTH, D_MODEL, D_MODEL), D_MODEL ** -0.5)
    w_ff1 = nrm(ks[15], (DEPTH, D_MODEL, D_FF), D_MODEL ** -0.5)
    w_ff2 = nrm(ks[16], (DEPTH, D_FF, D_MODEL), D_FF ** -0.5)
    return {"x": x, "c": c, "ctx": ctx, "c_ctx": c_ctx, "w_ada": w_ada, "b_ada": b_ada, "norm_g": norm_g,
            "w_in": w_in, "ret_decay": ret_decay, "sc_conv": sc_conv, "cf_conv": cf_conv, "cf_ln": cf_ln,
            "w_branch": w_branch, "w_out": w_out, "w_ff1": w_ff1, "w_ff2": w_ff2}


def reference(x, c, ctx, c_ctx, w_ada, b_ada, norm_g, w_in, ret_decay, sc_conv, cf_conv, cf_ln,
              w_branch, w_out, w_ff1, w_ff2):
    n_tok = x.shape[1]
    rows = n_tok // GRID_W
    row = jnp.repeat(jnp.arange(rows, dtype=jnp.float32), GRID_W)
    col = jnp.tile(jnp.arange(GRID_W, dtype=jnp.float32), rows)
    freqs = ROPE_BASE ** (-jnp.arange(ROPE_PAIRS, dtype=jnp.float32) / ROPE_PAIRS)
    ang = jnp.concatenate([row[:, None] * freqs[None], col[:, None] * freqs[None]], axis=-1)
    cos, sin = jnp.cos(ang), jnp.sin(ang)
    xc = ctx
    for layer in range(DEPTH):
        mod = jax.nn.silu(c) @ w_ada[layer] + b_ada[layer]
        mod_c = jax.nn.silu(c_ctx) @ w_ada[layer] + b_ada[layer]
        x, xc = hybrid_layer(x, xc, mod, mod_c, w_in[layer], norm_g[layer], ret_decay[layer], sc_conv[layer],
                             cf_conv[layer], cf_ln[layer], w_branch[layer], w_out[layer], w_ff1[layer],
                             w_ff2[layer], cos, sin, layer < DEPTH - 1)
    return x
```

```python
import contextlib
import math
import numpy as np
import ml_dtypes
import concourse.bass as bass
import concourse.mybir as mybir
from concourse.bass_utils import run_bass_kernel_spmd

F32 = mybir.dt.float32
BF16 = mybir.dt.bfloat16
AF = mybir.ActivationFunctionType
ALU = mybir.AluOpType
AX = mybir.AxisListType

D = 2048
DEPTH = 2
LT = 2048
CT = 256
T = LT + CT
NT = T // 128
NLT = LT // 128
BW = 512
INC = 13312
DFF = 8192
EPS = 1e-6
NCORES = 8
DEBUG = False


class Ev:
    __slots__ = ("sem", "val", "key")

    def __init__(self, sem, val, key):
        self.sem, self.val, self.key = sem, val, key

    def resolve(self):
        return self


class LazyEv:
    __slots__ = ("reg",)

    def __init__(self, reg):
        self.reg = reg

    def resolve(self):
        if self.reg.dsem is None:
            return None
        return Ev(self.reg.dsem, self.reg.dcnt, id(self.reg.dsem))


class Reg:
    def __init__(self, name, dram=False):
        self.name = name
        self.dram = dram
        self.w = {}
        self.r = {}
        self.dsem = None
        self.dcnt = 0
        self.dq = None


class Eng:
    def __init__(self, K, name, h):
        self.K, self.name, self.h = K, name, h
        self.sem = None
        self.cnt = 0
        self.seen = {}
        self.nsem = 0

    def wait(self, ev):
        if ev is None:
            return
        ev = ev.resolve()
        if ev is None:
            return
        if self.seen.get(ev.key, 0) >= ev.val:
            return
        self.h.wait_ge(ev.sem, ev.val)
        self.seen[ev.key] = ev.val

    def emit(self, ins):
        if self.sem is None or self.cnt >= 30000:
            self.sem = self.K.newsem(f"e_{self.name}_{self.nsem}")
            self.nsem += 1
            self.cnt = 0
        self.cnt += 1
        ins.then_inc(self.sem, 1)
        return Ev(self.sem, self.cnt, id(self.sem))


class KB:
    def __init__(self, nc, es):
        self.nc, self.es = nc, es
        self.pe = Eng(self, "pe", nc.tensor)
        self.act = Eng(self, "act", nc.scalar)
        self.dve = Eng(self, "dve", nc.vector)
        self.pool = Eng(self, "pool", nc.gpsimd)
        self.sp = Eng(self, "sp", nc.sync)
        self.nsems = 0
        self.same_engine_wait = True
        self.uid = 0
        self.free_sems = {}

    def newsem(self, name):
        self.nsems += 1
        return self.es.enter_context(self.nc.semaphore(f"{name}_{self.nsems}"))

    def sb(self, name, shape, dt, es=None):
        self.uid += 1
        return (es or self.es).enter_context(self.nc.sbuf_tensor(f"{name}_{self.uid}", list(shape), dt))

    def ps(self, name, shape, dt):
        return self.es.enter_context(self.nc.psum_tensor(name, list(shape), dt))

    def _pre(self, eng, reads, writes, pwrites):
        for r in reads:
            for ev in r.w.values():
                eng.wait(ev)
        for w in writes:
            for ev in w.w.values():
                eng.wait(ev)
            for ev in w.r.values():
                eng.wait(ev)
        for w in pwrites:
            for ev in w.r.values():
                eng.wait(ev)

    def _post(self, ev, reads, writes, pwrites):
        for r in reads:
            r.r[ev.key] = ev
        for w in writes:
            w.w = {ev.key: ev}
            w.r = {}
        for w in pwrites:
            w.w = {ev.key: ev}
            w.r = {}

    def op(self, eng, fn, reads=(), writes=(), pwrites=()):
        self._pre(eng, reads, writes, pwrites)
        ins = fn()
        ev = eng.emit(ins)
        if not self.same_engine_wait or eng is self.pe:
            eng.seen[ev.key] = ev.val
        self._post(ev, reads, writes, pwrites)
        return ev

    def mmg(self, out, pairs, reads=(), writes=(), start=True, stop=True):
        pe = self.pe
        self._pre(pe, reads, writes if start else (), ())
        n = len(pairs)
        ins = None
        for i, (l, r) in enumerate(pairs):
            ins = self.nc.tensor.matmul(out, l, r, start=(start and i == 0), stop=(stop and i == n - 1))
        ev = pe.emit(ins)
        pe.seen[ev.key] = ev.val
        self._post(ev, reads, writes, ())
        return ev

    def tr(self, out, in_, ident, reads=(), writes=(), pwrites=()):
        return self.op(self.pe, lambda: self.nc.tensor.transpose(out, in_, ident),
                       reads=reads, writes=writes, pwrites=pwrites)

    def dma(self, q, out, in_, src=(), dst=None, partial=False):
        store = dst.dram and len(src) > 0 and not src[0].dram
        owner = src[0] if store else dst
        assert not owner.dram, (owner.name, dst.name)
        for r in src:
            for ev in r.w.values():
                q.wait(ev)
        for ev in dst.r.values():
            q.wait(ev)
        for ev in dst.w.values():
            if not (partial and isinstance(ev, LazyEv)):
                q.wait(ev)
        if owner.dsem is None:
            fl = self.free_sems.setdefault(q.name, [])
            if fl:
                owner.dsem, owner.dcnt = fl.pop()
            else:
                owner.dsem = self.newsem("d_" + owner.name)
            owner.dq = q.name
        assert owner.dq == q.name, (owner.name, owner.dq, q.name)
        ins = q.h.dma_start(out=out, in_=in_)
        owner.dcnt += 16
        ins.then_inc(owner.dsem, 16)
        lev = LazyEv(owner)
        for r in src:
            r.r[id(owner)] = lev
        if partial:
            dst.w[id(owner)] = lev
        else:
            dst.w = {id(owner): lev}
        dst.r = {}
        return lev


class Tile:
    def __init__(self, K, name, shape, dt, es=None, psum=False):
        self.t = K.ps(name, shape, dt) if psum else K.sb(name, shape, dt, es)
        self.R = Reg(name)

    def __getitem__(self, k):
        return self.t[k]


class Ring:
    def __init__(self, tiles):
        self.tiles = tiles
        self.i = 0

    def next(self):
        t = self.tiles[self.i % len(self.tiles)]
        self.i += 1
        return t


def _bf(a):
    return np.ascontiguousarray(a).astype(ml_dtypes.bfloat16)


def _host_consts(hf):
    c = {}
    c["ident"] = _bf(np.eye(128, dtype=np.float32))
    c["ones_f"] = np.ones((128, 128), np.float32)
    tok = hf * LT + np.arange(LT)
    row = (tok // 64).astype(np.float32)
    col = (tok % 64).astype(np.float32)
    freqs = (np.float32(10000.0) ** (-np.arange(32, dtype=np.float32) / np.float32(32))).astype(np.float32)
    ang32 = np.concatenate([row[:, None] * freqs[None], col[:, None] * freqs[None]], -1).astype(np.float32)
    cs = np.stack([np.cos(ang32.astype(np.float64)), np.sin(ang32.astype(np.float64))], 1)
    cs = cs.reshape(NLT, 128, 2, 64).transpose(1, 0, 2, 3)
    c["ropeq"] = np.ascontiguousarray(cs).astype(np.float32)
    c["ropek"] = np.ascontiguousarray(cs * (128.0 ** -0.5)).astype(np.float32)
    m = np.arange(128)[:, None].astype(np.float32)
    n = np.arange(128)[None, :].astype(np.float32)
    msk = np.stack([np.maximum(n - m, 0), np.maximum(m - n, 0), (n >= m).astype(np.float32),
                    (m > n).astype(np.float32)], 1)
    c["msk"] = np.ascontiguousarray(msk).astype(np.float32)
    p = np.arange(128, dtype=np.float32)
    c["pcol"] = np.stack([127 - p, p, p + 1, 128 - p], 1).astype(np.float32)
    c["flags"] = np.tile(np.array([[hf, 1 - hf]], np.float32), (128, 1))
    L = 4096
    kk = hf * LT + np.arange(LT, dtype=np.int64)
    nn = np.arange(L, dtype=np.int64)
    r = (nn[:, None] * kk[None, :]) % L
    th = 2.0 * np.pi * r.astype(np.float64) / L
    sc = 1.0 / math.sqrt(L * 128.0)
    c["CL"] = _bf(np.cos(th) * sc)
    c["SL"] = _bf(np.sin(th) * sc)
    j = np.arange(128, dtype=np.int64)
    th2 = 2.0 * np.pi * ((j[:, None] * j[None, :]) % 128).astype(np.float64) / 128
    c["C128"] = _bf(np.cos(th2))
    c["nS128"] = _bf(-np.sin(th2))
    q = np.arange(256, dtype=np.int64)
    th3 = 2.0 * np.pi * ((q[:, None] * q[None, :]) % 256).astype(np.float64) / 256
    sc3 = 1.0 / math.sqrt(256 * 128.0)
    c["C256"] = _bf(np.cos(th3) * sc3)
    c["S256"] = _bf(np.sin(th3) * sc3)
    return c


CONST_SPECS = {
    "ident": ([128, 128], BF16), "ones_f": ([128, 128], F32),
    "ropeq": ([128, 16, 2, 64], F32), "ropek": ([128, 16, 2, 64], F32),
    "msk": ([128, 4, 128], F32), "pcol": ([128, 4], F32), "flags": ([128, 2], F32),
    "CL": ([4096, 2048], BF16), "SL": ([4096, 2048], BF16),
    "C128": ([128, 128], BF16), "nS128": ([128, 128], BF16),
    "C256": ([256, 256], BF16), "S256": ([256, 256], BF16),
}

IN_SPECS = {
    "x_in": ([LT, D], F32), "ctx_in": ([CT, D], F32), "cvec": ([128, 16, 2], F32),
    "w_ada": ([DEPTH, D, 6 * D], F32), "b_col": ([DEPTH, 128, 96], F32), "b_row": ([DEPTH, 1, 6 * D], F32),
    "ng_col": ([DEPTH, 128, 4, 16], F32), "ng_row": ([DEPTH, 4, D], F32),
    "w_in": ([DEPTH, D, INC], F32), "ret_decay": ([DEPTH, 128, 8], F32),
    "sc_w": ([DEPTH, 128, 4, 3], F32), "cf_w": ([DEPTH, 128, 4, 31], F32), "cf_ln": ([DEPTH, 128, 4, 2], F32),
    "w_branch": ([DEPTH, 4, BW, D], F32), "w_out": ([DEPTH, D, D], F32),
    "w_ff1": ([DEPTH, D, DFF], F32), "w_ff2": ([DEPTH, DFF, D], F32),
}


class Prog:
    def __init__(self, debug=False, stop_after=None, nlayers=DEPTH, ncores=NCORES, nocoll=False):
        self.debug, self.stop_after, self.nlayers = debug, stop_after, nlayers
        self.nocoll = nocoll
        self.skipB = False
        self.cstop = None
        self.groups = [[2 * i, 2 * i + 1] for i in range(ncores // 2)]
        self.nc = bass.Bass("TRN2", target_bir_lowering=False)
        self.es = contextlib.ExitStack()

    def dram(self, name, shape, dt, dbg=True):
        kind = "ExternalOutput" if (self.debug and dbg) else "Internal"
        return self.nc.dram_tensor(name, list(shape), dt, kind=kind).ap()

    def build(self):
        nc = self.nc
        with self.es as es:
            self.K = K = KB(nc, es)
            self.I = I = {}
            for name, (shape, dt) in {**IN_SPECS, **CONST_SPECS}.items():
                I[name] = nc.dram_tensor(name, list(shape), dt, kind="ExternalInput").ap()
            self.out = nc.dram_tensor("out", [LT, D], F32, kind="ExternalOutput").ap()
            self.R_out = Reg("out", dram=True)
            self.XS = self.dram("XS", [T, D], F32); self.R_XS = Reg("XS", dram=True)
            self.SK = self.dram("SK", [T, 2048], BF16); self.R_SK = Reg("SK", dram=True)
            self.SUC = self.dram("SUC", [CT, 512], BF16); self.R_SUC = Reg("SUC", dram=True)
            self.SF = self.dram("SF", [2560, T], BF16); self.R_SF = Reg("SF", dram=True)
            self.SG = self.dram("SG", [8192, T], BF16); self.R_SG = Reg("SG", dram=True)
            self.SY = self.dram("SY", [2048, T], BF16); self.R_SY = Reg("SY", dram=True)
            self.EXU = self.dram("EXU", [LT, 512], BF16, dbg=False); self.R_EXU = Reg("EXU", dram=True)
            self.EXUG = self.dram("EXUG", [2 * LT, 512], BF16, dbg=False); self.R_EXUG = Reg("EXUG", dram=True)
            self.EXS = self.dram("EXS", [1024, 128], F32, dbg=False); self.R_EXS = Reg("EXS", dram=True)
            self.EXSG = self.dram("EXSG", [2048, 128], F32, dbg=False); self.R_EXSG = Reg("EXSG", dram=True)
            self.EXH = self.dram("EXH", [128, 640], BF16, dbg=False); self.R_EXH = Reg("EXH", dram=True)
            self.EXHG = self.dram("EXHG", [256, 640], BF16, dbg=False); self.R_EXHG = Reg("EXHG", dram=True)
            self.cc_sem = K.newsem("cc")
            self.ccdummy = Tile(K, "ccdummy", [128, 4], F32)
            self.junk = Tile(K, "junk", [128, D], BF16)
            self.cc_cnt = 0
            self.P = Ring([Tile(K, f"P{i}", [128, 512], F32, psum=True) for i in range(6)])
            self.PT = Ring([Tile(K, f"PT{i}", [128, 1024], BF16, psum=True) for i in range(2)])
            self.ident = self.const_load("ident", [128, 128], BF16)
            self.ones_f = self.const_load("ones_f", [128, 128], F32)
            self.flags = self.const_load("flags", [128, 2], F32)
            self.pcol = self.const_load("pcol", [128, 4], F32)
            self.epsc = Tile(K, "epsc", [128, 1], F32)
            K.op(K.dve, lambda: nc.vector.memset(self.epsc[:], EPS), writes=[self.epsc.R])
            self.cs = Tile(K, "cs", [128, 16, 2], BF16)
            cv = self.const_load("cvec", [128, 16, 2], F32)
            K.op(K.act, lambda: nc.scalar.activation(out=self.cs[:], in_=cv[:], func=AF.Silu),
                 reads=[cv.R], writes=[self.cs.R])
            self.modc = Tile(K, "modc", [128, 6, 16, 2], F32)
            self.MA = Tile(K, "MA", [128, 2, 16, 2], F32)

            for l in range(self.nlayers):
                self.phase_P(l)
                if self.stop_after == f"P{l}":
                    break
                self.phase_A(l)
                if self.stop_after == f"A{l}":
                    break
                if not self.skipB:
                    self.phase_B(l)
                if self.stop_after and self.stop_after.startswith("B") and self.stop_after.endswith(str(l)):
                    break
                self.phase_C(l)
                if self.stop_after == f"C{l}":
                    break
            self.finish()
        return nc

    def const_load(self, name, shape, dt, es=None, src=None):
        K = self.K
        t = Tile(K, name, shape, dt, es)
        K.dma(K.sp, t[:], self.I[name] if src is None else src, dst=t.R)
        return t

    def finish(self):
        K = self.K
        for R in (self.R_out, self.R_XS, self.R_SK, self.R_SUC, self.R_SF, self.R_SG, self.R_SY,
                  self.R_EXU, self.R_EXS, self.R_EXH, self.R_EXUG, self.R_EXSG, self.R_EXHG):
            for ev in list(R.w.values()) + list(R.r.values()):
                K.sp.wait(ev)
        for e in (K.pe, K.act, K.dve, K.pool):
            if e.sem is not None:
                K.sp.wait(Ev(e.sem, e.cnt, id(e.sem)))

    def collective(self, src_ap, R_src, dst_ap, R_dst):
        K, nc = self.K, self.nc
        pool = K.pool
        if self.nocoll:
            rows = src_ap.shape[0]
            pp = min(rows, 128)
            for half in range(2):
                K.dma(K.sp, dst_ap[half * rows:(half + 1) * rows, :].rearrange("(p r) c -> p (r c)", p=pp),
                      src_ap.rearrange("(p r) c -> p (r c)", p=pp), src=[R_src], dst=self.ccdummy.R, partial=(half == 1))
            for ev in self.ccdummy.R.w.values():
                K.sp.wait(ev)
            ev = K.op(K.dve, lambda: nc.vector.memset(self.ccdummy[:], 0.0), writes=[self.ccdummy.R])
            R_src.r[ev.key] = ev
            R_dst.w = {ev.key: ev}
            R_dst.r = {}
            return
        K._pre(pool, [R_src], [R_dst], [])
        ins = nc.gpsimd.collective_compute(
            "AllGather", ALU.bypass, replica_groups=self.groups,
            ins=[src_ap.opt()], outs=[dst_ap.opt()])
        self.cc_cnt += 1
        ins.then_inc(self.cc_sem, 1)
        nc.gpsimd.wait_ge(self.cc_sem, self.cc_cnt)
        ev = pool.emit(nc.gpsimd.memset(self.ccdummy[:], 0.0))
        R_src.r[ev.key] = ev
        R_dst.w = {ev.key: ev}
        R_dst.r = {}

    class WStream:
        def __init__(self, prog, loads, depth=2, ring=None):
            self.prog, self.loads, self.depth = prog, loads, depth
            self.ring = ring if ring is not None else prog.WS
            self.issued = 0
            self.slots = {}

        def get(self, i):
            while self.issued < min(len(self.loads), i + self.depth + 1):
                slot = self.ring.next()
                self.loads[self.issued](slot)
                self.slots[self.issued] = slot
                self.issued += 1
            return self.slots.pop(i)

    def wload(self, slot, src_ap, nk=16, ncol=512):
        K = self.K
        K.dma(K.pool, slot[:, 0:nk, 0:ncol], src_ap.rearrange("(k p) c -> p k c", p=128), dst=slot.R)

    def alloc_ws(self, es2, n=3):
        self.WS = Ring([Tile(self.K, f"ws{i}", [128, 16, 512], BF16, es2) for i in range(n)])
        return [t.R for t in self.WS.tiles]

    def phase_P(self, l):
        K, nc, I = self.K, self.nc, self.I
        with contextlib.ExitStack() as es2:
            wsr = self.alloc_ws(es2)
            bcol = self.const_load("b_col", [128, 96], F32, es2, src=I["b_col"][l])
            ngcol = self.const_load("ng_col", [128, 4, 16], F32, es2, src=I["ng_col"][l])
            cbs = [cb for cb in range(24) if cb // 4 in (0, 1, 3, 4)]
            loads = [(lambda slot, cb=cb: self.wload(slot, I["w_ada"][l, :, cb * 512:(cb + 1) * 512])) for cb in cbs]
            st = self.WStream(self, loads)
            cs = self.cs
            for i, cb in enumerate(cbs):
                wa = st.get(i)
                v, j4 = cb // 4, cb % 4
                for j in range(4):
                    bank = self.P.next()
                    K.mmg(bank[:, 0:2], [(wa[:, kc, j * 128:(j + 1) * 128], cs[:, kc, :]) for kc in range(16)],
                          reads=[wa.R, cs.R], writes=[bank.R])
                    idx = v * 16 + j4 * 4 + j
                    K.op(K.dve, lambda bank=bank, idx=idx, v=v, dc=j4 * 4 + j: nc.vector.tensor_scalar(
                        out=self.modc[:, v, dc, :], in0=bank[:, 0:2], scalar1=bcol[:, idx:idx + 1],
                        scalar2=None, op0=ALU.add), reads=[bank.R, bcol.R], pwrites=[self.modc.R])
            for a, (vsc, gi) in enumerate(((1, 0), (4, 2))):
                K.op(K.dve, lambda a=a, vsc=vsc, gi=gi: nc.vector.scalar_tensor_tensor(
                    out=self.MA[:, a, :, :], in0=self.modc[:, vsc, :, :], scalar=1.0,
                    in1=ngcol[:, gi, :].unsqueeze(2).to_broadcast([128, 16, 2]),
                    op0=ALU.add, op1=ALU.mult), reads=[self.modc.R, ngcol.R], pwrites=[self.MA.R])
            self.drain([bcol.R, ngcol.R] + wsr)

    def phase_Prow(self, l, r, ROW, es2):
        K, nc, I = self.K, self.nc, self.I
        rtmp = Ring([Tile(K, f"rtmp{i}", [1, 3, 512], F32, es2) for i in range(2)])
        cbs = [cb for cb in range(24) if cb // 4 in (2, 5)]
        loads = [(lambda slot, cb=cb: self.wload(slot, I["w_ada"][l, :, cb * 512:(cb + 1) * 512])) for cb in cbs]
        st = self.WStream(self, loads)
        cs = self.cs
        for i, cb in enumerate(cbs):
            wa = st.get(i)
            v, j4 = cb // 4, cb % 4
            vi = 0 if v == 2 else 1
            rt = rtmp.next()
            K.dma(K.sp, rt[0:1, 1, :], I["b_row"][l, :, v * D + j4 * 512: v * D + (j4 + 1) * 512], dst=rt.R)
            K.dma(K.sp, rt[0:1, 2, :], I["ng_row"][l, 1 + 2 * vi:2 + 2 * vi, j4 * 512:(j4 + 1) * 512],
                  dst=rt.R, partial=True)
            bank = self.P.next()
            K.mmg(bank[0:1, :], [(cs[:, kc, r:r + 1], wa[:, kc, :]) for kc in range(16)],
                  reads=[wa.R, cs.R], writes=[bank.R])
            K.op(K.dve, lambda bank=bank, rt=rt: nc.vector.tensor_tensor(
                out=rt[0:1, 0, :], in0=bank[0:1, :], in1=rt[0:1, 1, :], op=ALU.add),
                reads=[bank.R, rt.R], writes=[rt.R])
            K.op(K.dve, lambda rt=rt: nc.vector.tensor_tensor(
                out=rt[0:1, 0, :], in0=rt[0:1, 0, :], in1=rt[0:1, 2, :], op=ALU.mult),
                reads=[rt.R], writes=[rt.R])
            bank2 = self.P.next()
            K.mmg(bank2[:, :], [(self.ones_f[0:1, :], rt[0:1, 0, :])],
                  reads=[self.ones_f.R, rt.R], writes=[bank2.R])
            row = ROW[vi]
            K.op(K.act, lambda bank2=bank2, row=row, j4=j4: nc.scalar.copy(
                out=row[:, j4 * 512:(j4 + 1) * 512], in_=bank2[:, :]),
                reads=[bank2.R], pwrites=[row.R])
        return [t.R for t in rtmp.tiles]

    def drain(self, regs):
        K = self.K
        evs = []
        for R in regs:
            evs.extend(R.w.values())
            evs.extend(R.r.values())
        for e in (K.pe, K.act, K.dve, K.pool, K.sp):
            for ev in evs:
                e.wait(ev)
        for R in regs:
            if R.dsem is not None:
                if R.dcnt < 40000:
                    K.free_sems.setdefault(R.dq, []).append((R.dsem, R.dcnt))
                R.dsem = None
            R.w = {}
            R.r = {}

    def norm_mod_T(self, xt, a_idx, s_idx, kind, hT, toff, xn, ssr, eng_sq=None):
        K, nc = self.K, self.nc
        junk = self.junk
        K.op(K.act, lambda: nc.scalar.activation(out=junk[:], in_=xt[:], func=AF.Square, accum_out=ssr[:, 0:1]),
             reads=[xt.R], writes=[junk.R, ssr.R])
        K.op(K.act, lambda: nc.scalar.activation(out=ssr[:, 1:2], in_=ssr[:, 0:1], func=AF.Sqrt, scale=1.0 / D,
                                                 bias=self.epsc[:, 0:1]), reads=[ssr.R, self.epsc.R], writes=[ssr.R])
        K.op(K.dve, lambda: nc.vector.reciprocal(out=ssr[:, 2:3], in_=ssr[:, 1:2]), reads=[ssr.R], writes=[ssr.R])
        K.op(K.dve, lambda: nc.vector.tensor_scalar(out=xn[:], in0=xt[:], scalar1=ssr[:, 2:3], scalar2=None,
                                                    op0=ALU.mult), reads=[ssr.R, xt.R], writes=[xn.R])
        for half in range(2):
            pt = self.PT.next()
            for j in range(8):
                dc = half * 8 + j
                K.tr(pt[:, j * 128:(j + 1) * 128], xn[:, dc * 128:(dc + 1) * 128], self.ident[:],
                     reads=[xn.R, self.ident.R], writes=[pt.R] if j == 0 else (), pwrites=() if j == 0 else [pt.R])
            ptv = pt[:].rearrange("p (a b) -> p a b", a=8)
            dst = hT[:, half * 8:(half + 1) * 8, toff:toff + 128]
            K.op(K.dve, lambda ptv=ptv, dst=dst, half=half: nc.vector.tensor_tensor(
                out=dst, in0=ptv, in1=self.MA[:, a_idx, half * 8:(half + 1) * 8, kind:kind + 1].to_broadcast([128, 8, 128]),
                op=ALU.mult), reads=[pt.R, self.MA.R], pwrites=[hT.R])
            K.op(K.dve, lambda dst=dst, half=half: nc.vector.tensor_tensor(
                out=dst, in0=dst, in1=self.modc[:, s_idx, half * 8:(half + 1) * 8, kind:kind + 1].to_broadcast([128, 8, 128]),
                op=ALU.add), reads=[self.modc.R, hT.R], pwrites=[hT.R])

    def x_src(self, l, gt):
        if l == 0:
            if gt < NLT:
                return self.I["x_in"][gt * 128:(gt + 1) * 128, :], None
            return self.I["ctx_in"][(gt - NLT) * 128:(gt - NLT + 1) * 128, :], None
        return self.XS[gt * 128:(gt + 1) * 128, :], self.R_XS

    def rope(self, bank, tab, gt, out, rtmp):
        K, nc = self.K, self.nc
        psv = bank[:, :].rearrange("p (h a j) -> p h a j", h=4, a=2)
        ov = out[:, :].rearrange("p (h a j) -> p h a j", h=4, a=2)
        t1, t2 = psv[:, :, 0, :], psv[:, :, 1, :]
        cosb = tab[:, gt, 0:1, :].to_broadcast([128, 4, 64])
        sinb = tab[:, gt, 1:2, :].to_broadcast([128, 4, 64])
        for i, (a, b) in enumerate(((t1, cosb), (t2, sinb), (t1, sinb), (t2, cosb))):
            K.op(K.dve, lambda a=a, b=b, i=i: nc.vector.tensor_tensor(out=rtmp[:, i, :, :], in0=a, in1=b, op=ALU.mult),
                 reads=[bank.R, tab.R], writes=[rtmp.R] if i == 0 else (), pwrites=() if i == 0 else [rtmp.R])
        K.op(K.dve, lambda: nc.vector.tensor_tensor(out=ov[:, :, 0, :], in0=rtmp[:, 0, :, :], in1=rtmp[:, 1, :, :],
                                                    op=ALU.subtract), reads=[rtmp.R], writes=[out.R])
        K.op(K.dve, lambda: nc.vector.tensor_tensor(out=ov[:, :, 1, :], in0=rtmp[:, 2, :, :], in1=rtmp[:, 3, :, :],
                                                    op=ALU.add), reads=[rtmp.R], pwrites=[out.R])

    def phase_A(self, l):
        K, nc, I = self.K, self.nc, self.I
        last = (l == DEPTH - 1)
        blocks = [(tb * 4, 4) for tb in range(4)] + [(NLT, 2)]
        with contextlib.ExitStack() as es2:
            ropeq = self.const_load("ropeq", [128, 16, 2, 64], F32, es2)
            ropek = self.const_load("ropek", [128, 16, 2, 64], F32, es2)
            xts = Ring([Tile(K, f"xt{i}", [128, D], F32, es2) for i in range(2)])
            xn = Tile(K, "xn", [128, D], BF16, es2)
            ssr = Tile(K, "ssr", [128, 4], F32, es2)
            hT = Tile(K, "hT", [128, 16, 512], BF16, es2)
            rtmp = Tile(K, "ropetmp", [128, 4, 4, 64], F32, es2)
            stg = Ring([Tile(K, f"stg{i}", [128, 512], BF16, es2) for i in range(4)])
            wsr = self.alloc_ws(es2)
            HB = Tile(K, "HB", [128, 20, 32], BF16, es2)
            loads = []
            plan = []
            for bi, (t0, nt) in enumerate(blocks):
                isctx = t0 >= NLT
                cbs = list(range(26)) if not (isctx and last) else [0, 1]
                for cb in cbs:
                    plan.append((bi, cb))
                    loads.append(lambda slot, cb=cb: self.wload(slot, I["w_in"][l, :, cb * 512:(cb + 1) * 512]))
            st = self.WStream(self, loads)
            li = 0
            cur_bi = -1
            for (bi, cb) in plan:
                t0, nt = blocks[bi]
                ntok = nt * 128
                isctx = t0 >= NLT
                kind = 1 if isctx else 0
                if bi != cur_bi:
                    cur_bi = bi
                    for t in range(nt):
                        gt = t0 + t
                        xt = xts.next()
                        src, Rs = self.x_src(l, gt)
                        K.dma(K.sp, xt[:], src, src=[Rs] if Rs else [], dst=xt.R)
                        self.norm_mod_T(xt, 0, 0, kind, hT, t * 128, xn, ssr)
                w = st.get(li)
                li += 1
                if cb < 5:
                    for t in range(nt):
                        gt = t0 + t
                        bank = self.P.next()
                        K.mmg(bank[:, :], [(hT[:, kc, t * 128:(t + 1) * 128], w[:, kc, :]) for kc in range(16)],
                              reads=[hT.R, w.R], writes=[bank.R])
                        o = stg.next()
                        if cb in (0, 2) and not isctx:
                            self.rope(bank, ropek if cb == 0 else ropeq, gt, o, rtmp)
                        elif cb == 0:
                            K.op(K.act, lambda bank=bank, o=o: nc.scalar.mul(out=o[:, :], in_=bank[:, :], mul=128.0 ** -0.5),
                                 reads=[bank.R], writes=[o.R])
                        elif cb == 3:
                            K.op(K.act, lambda bank=bank, o=o: nc.scalar.activation(out=o[:, :], in_=bank[:, :], func=AF.Silu),
                                 reads=[bank.R], writes=[o.R])
                        else:
                            K.op(K.act, lambda bank=bank, o=o: nc.scalar.copy(out=o[:, :], in_=bank[:, :]),
                                 reads=[bank.R], writes=[o.R])
                        if cb < 4:
                            K.dma(K.sp, self.SK[gt * 128:(gt + 1) * 128, cb * 512:(cb + 1) * 512], o[:, :],
                                  src=[o.R], dst=self.R_SK, partial=True)
                        elif not isctx:
                            K.dma(K.sp, self.EXU[gt * 128:(gt + 1) * 128, :], o[:, :], src=[o.R], dst=self.R_EXU, partial=True)
                        else:
                            K.dma(K.sp, self.SUC[(gt - NLT) * 128:(gt - NLT + 1) * 128, :], o[:, :],
                                  src=[o.R], dst=self.R_SUC, partial=True)
                else:
                    for j in range(4):
                        bank = self.P.next()
                        K.mmg(bank[:, 0:ntok], [(w[:, kc, j * 128:(j + 1) * 128], hT[:, kc, 0:ntok]) for kc in range(16)],
                              reads=[hT.R, w.R], writes=[bank.R])
                        o = stg.next()
                        fn = AF.Copy if cb < 10 else AF.Sigmoid
                        K.op(K.act, lambda bank=bank, o=o, fn=fn: nc.scalar.activation(out=o[:, 0:ntok], in_=bank[:, 0:ntok], func=fn),
                             reads=[bank.R], writes=[o.R])
                        if cb < 10:
                            r0 = (cb - 5) * 512 + j * 128
                            K.dma(K.sp, self.SF[r0:r0 + 128, t0 * 128:t0 * 128 + ntok], o[:, 0:ntok],
                                  src=[o.R], dst=self.R_SF, partial=True)
                            a = (cb - 5) * 4 + j
                            if t0 == 0:
                                K.op(K.act, lambda o=o, a=a: nc.scalar.copy(out=HB[:, a, 0:16], in_=o[:, 0:16]),
                                     reads=[o.R], pwrites=[HB.R])
                            if t0 == NLT - 4:
                                K.op(K.act, lambda o=o, a=a: nc.scalar.copy(out=HB[:, a, 16:32], in_=o[:, 496:512]),
                                     reads=[o.R], pwrites=[HB.R])
                        else:
                            r0 = (cb - 10) * 512 + j * 128
                            K.dma(K.sp, self.SG[r0:r0 + 128, t0 * 128:t0 * 128 + ntok], o[:, 0:ntok],
                                  src=[o.R], dst=self.R_SG, partial=True)
            K.dma(K.sp, self.EXH, HB[:, :, :].rearrange("p a w -> p (a w)"), src=[HB.R], dst=self.R_EXH)
            self.drain([t.R for t in (ropeq, ropek, xn, ssr, hT, rtmp, HB)] + [t.R for t in xts.tiles + stg.tiles] + wsr)

    def mms(self, bank, triples, reads):
        K, nc = self.K, self.nc
        pe = K.pe
        K._pre(pe, reads, [bank.R], ())
        ins = None
        for (o, lt, r) in triples:
            ins = nc.tensor.matmul(o, lt, r, start=True, stop=True)
        ev = pe.emit(ins)
        pe.seen[ev.key] = ev.val
        K._post(ev, reads, [bank.R], ())
        return ev

    def phase_B(self, l):
        self.collective(self.EXU, self.R_EXU, self.EXUG, self.R_EXUG)
        self.collective(self.EXH, self.R_EXH, self.EXHG, self.R_EXHG)
        self.phase_B1(l)
        if self.stop_after == f"B1{l}":
            return
        self.phase_B2(l)
        if self.stop_after == f"B2{l}":
            return
        self.phase_B34(l)

    def phase_B1(self, l):
        K, nc, I = self.K, self.nc, self.I
        first = (l == 0)
        ychunks = list(range(NT)) if first else list(range(NLT))
        HS = [slice(h * 128, (h + 1) * 128) for h in range(4)]
        with contextlib.ExitStack() as es2:
            Kt = Tile(K, "Kt", [128, NT, 512], BF16, es2)
            Vt = Tile(K, "Vt", [128, NT, 512], BF16, es2)
            KLb = Tile(K, "KLb", [128, NT, 512], BF16, es2)
            kT = Tile(K, "kT", [128, 4, T], BF16, es2)
            qT = Tile(K, "qT", [128, 4, T], BF16, es2)
            Sbf = Tile(K, "Sbf", [128, 2, NT, 512], BF16, es2)
            Sc = Tile(K, "Sc", [128, 2, 512], F32, es2)
            sctx = Tile(K, "sctx", [128, 2, 512], F32, es2)
            tmpS = Tile(K, "tmpS", [128, 512], F32, es2)
            TG = Tile(K, "TG", [128, 2, 512], F32, es2)
            dec = Tile(K, "dec", [128, 64], F32, es2)
            mskt = self.const_load("msk", [128, 4, 128], F32, es2)
            mtmp = Tile(K, "mtmp", [128, 2, 128], F32, es2)
            M = Tile(K, "M", [128, 4, 128], BF16, es2)
            Qs = Ring([Tile(K, f"Qs{i}", [128, 2, 512], BF16, es2) for i in range(2)])
            Gs = Ring([Tile(K, f"Gs{i}", [128, 512], BF16, es2) for i in range(2)])
            Pm = Ring([Tile(K, f"Pm{i}", [128, 512], BF16, es2) for i in range(2)])
            ysb = Ring([Tile(K, f"ysb{i}", [128, 512], F32, es2) for i in range(2)])
            t2 = Ring([Tile(K, f"t2{i}", [128, 512], F32, es2) for i in range(2)])
            sq = Tile(K, "sq", [128, 512], F32, es2)
            stt = Ring([Tile(K, f"stt{i}", [128, 16], F32, es2) for i in range(2)])
            yo = Ring([Tile(K, f"yo{i}", [128, 512], BF16, es2) for i in range(2)])
            ystg = Ring([Tile(K, f"ystg{i}", [128, 4, 128], BF16, es2) for i in range(2)])

            K.dma(K.sp, Kt[:], self.SK[:, 0:512].rearrange("(t p) c -> p t c", p=128), src=[self.R_SK], dst=Kt.R)
            K.dma(K.sp, Vt[:], self.SK[:, 512:1024].rearrange("(t p) c -> p t c", p=128), src=[self.R_SK], dst=Vt.R)
            K.dma(K.sp, dec[:, 0:8], I["ret_decay"][l], dst=dec.R)
            A = lambda fn, **kw: K.op(K.act, fn, **kw)
            V = lambda fn, **kw: K.op(K.dve, fn, **kw)
            A(lambda: nc.scalar.activation(out=dec[:, 8:16], in_=dec[:, 0:8], func=AF.Exp, scale=-1.0), reads=[dec.R], writes=[dec.R])
            A(lambda: nc.scalar.activation(out=dec[:, 8:16], in_=dec[:, 8:16], func=AF.Ln, bias=1.0, scale=1.0), reads=[dec.R], writes=[dec.R])
            V(lambda: nc.vector.tensor_scalar(out=dec[:, 16:24], in0=dec[:, 8:16], scalar1=-1.0, scalar2=None, op0=ALU.mult), reads=[dec.R], writes=[dec.R])
            A(lambda: nc.scalar.activation(out=dec[:, 24:32], in_=dec[:, 16:24], func=AF.Exp, scale=128.0), reads=[dec.R], writes=[dec.R])
            for d_ in range(2):
                for h in range(4):
                    i = d_ * 4 + h
                    A(lambda i=i, d_=d_: nc.scalar.activation(out=dec[:, 32 + i:33 + i], in_=self.pcol[:, d_:d_ + 1], func=AF.Exp,
                                                               scale=dec[:, 16 + i:17 + i]), reads=[dec.R, self.pcol.R], writes=[dec.R])
                    A(lambda i=i, d_=d_: nc.scalar.activation(out=dec[:, 40 + i:41 + i], in_=self.pcol[:, 2 + d_:3 + d_], func=AF.Exp,
                                                               scale=dec[:, 16 + i:17 + i]), reads=[dec.R, self.pcol.R], writes=[dec.R])
            for d_ in range(2):
                V(lambda d_=d_: nc.vector.tensor_scalar(out=dec[:, 56 + 4 * d_:60 + 4 * d_], in0=dec[:, 16 + 4 * d_:20 + 4 * d_],
                                                         scalar1=self.flags[:, d_:d_ + 1], scalar2=None, op0=ALU.mult),
                  reads=[dec.R, self.flags.R], writes=[dec.R])
            A(lambda: nc.scalar.activation(out=dec[:, 48:56], in_=dec[:, 56:64], func=AF.Exp, scale=2048.0), reads=[dec.R], writes=[dec.R])
            for h in range(4):
                A(lambda h=h: nc.scalar.activation(out=mtmp[:, 0, :], in_=mskt[:, 0, :], func=AF.Exp, scale=dec[:, 16 + h:17 + h]),
                  reads=[dec.R, mskt.R], writes=[mtmp.R])
                V(lambda: nc.vector.tensor_tensor(out=mtmp[:, 0, :], in0=mtmp[:, 0, :], in1=mskt[:, 2, :], op=ALU.mult), reads=[mtmp.R, mskt.R], writes=[mtmp.R])
                A(lambda h=h: nc.scalar.activation(out=mtmp[:, 1, :], in_=mskt[:, 1, :], func=AF.Exp, scale=dec[:, 20 + h:21 + h]),
                  reads=[dec.R, mskt.R, mtmp.R], writes=[mtmp.R])
                V(lambda: nc.vector.tensor_tensor(out=mtmp[:, 1, :], in0=mtmp[:, 1, :], in1=mskt[:, 3, :], op=ALU.mult), reads=[mtmp.R, mskt.R], writes=[mtmp.R])
                V(lambda h=h: nc.vector.tensor_tensor(out=M[:, h, :], in0=mtmp[:, 0, :], in1=mtmp[:, 1, :], op=ALU.add), reads=[mtmp.R], writes=[M.R])

            stopx = self.stop_after if (self.stop_after or "").startswith("B1") else None
            npairs = (NT if first else NLT) // 2
            if stopx == f"B1a{l}":
                npairs = 0
            for pi in range(npairs):
                g0 = 2 * pi
                qs = Qs.next()
                K.dma(K.sp, qs[:], self.SK[g0 * 128:(g0 + 2) * 128, 1024:1536].rearrange("(t p) c -> p t c", p=128),
                      src=[self.R_SK], dst=qs.R)
                for (srcT, srcR, dstT) in ((None, Kt.R, kT), (qs, qs.R, qT)):
                    pt = self.PT.next()
                    for tt in range(2):
                        for h in range(4):
                            sl = slice((h * 2 + tt) * 128, (h * 2 + tt + 1) * 128)
                            in_ = Kt[:, g0 + tt, HS[h]] if srcT is None else srcT[:, tt, HS[h]]
                            firstw = (tt == 0 and h == 0)
                            K.tr(pt[:, sl], in_, self.ident[:], reads=[srcR, self.ident.R],
                                 writes=[pt.R] if firstw else (), pwrites=() if firstw else [pt.R])
                    K.op(K.act, lambda pt=pt, dstT=dstT, g0=g0: nc.scalar.copy(
                        out=dstT[:, :, g0 * 128:(g0 + 2) * 128].rearrange("p h (t n) -> p h t n", t=2),
                        in_=pt[:].rearrange("p (h t n) -> p h t n", h=4, t=2)), reads=[pt.R], pwrites=[dstT.R])
            for h in range(4 if stopx != f"B1a{l}" else 0):
                V(lambda h=h: nc.vector.tensor_scalar(out=KLb[:, :, HS[h]], in0=Kt[:, :, HS[h]], scalar1=dec[:, 36 + h:37 + h],
                                                       scalar2=None, op0=ALU.mult), reads=[Kt.R, dec.R], pwrites=[KLb.R])
            for h in range(4 if stopx != f"B1a{l}" else 0):
                V(lambda h=h: nc.vector.tensor_scalar(out=Kt[:, :, HS[h]], in0=Kt[:, :, HS[h]], scalar1=dec[:, 32 + h:33 + h],
                                                       scalar2=None, op0=ALU.mult), reads=[Kt.R, dec.R], writes=[Kt.R])
            KL = (Kt, KLb)

            def chain(d_, chunks, zero_init, store):
                if zero_init:
                    V(lambda: nc.vector.memset(Sc[:, d_, :], 0.0), pwrites=[Sc.R])
                for c in chunks:
                    if store:
                        A(lambda c=c: nc.scalar.copy(out=Sbf[:, d_, c, :], in_=Sc[:, d_, :]), reads=[Sc.R], pwrites=[Sbf.R])
                    bank = self.P.next()
                    self.mms(bank, [(bank[:, HS[h]], KL[d_][:, c, HS[h]], Vt[:, c, HS[h]]) for h in range(4)],
                             reads=[KL[d_].R, Vt.R])
                    V(lambda: nc.vector.tensor_tensor(
                        out=tmpS[:, :].rearrange("p (h v) -> p h v", h=4), in0=Sc[:, d_, :].rearrange("p (h v) -> p h v", h=4),
                        in1=dec[:, 24 + 4 * d_:28 + 4 * d_].unsqueeze(2).to_broadcast([128, 4, 128]), op=ALU.mult),
                      reads=[Sc.R, dec.R], writes=[tmpS.R])
                    V(lambda bank=bank: nc.vector.tensor_tensor(out=Sc[:, d_, :], in0=tmpS[:, :], in1=bank[:, :], op=ALU.add),
                      reads=[tmpS.R, bank.R], writes=[Sc.R])

            if stopx in (f"B1a{l}", f"B1b{l}"):
                ychunks = []
            chain(0, [NLT, NLT + 1], True, first)
            chain(1, [NLT + 1, NLT], True, first)
            V(lambda: nc.vector.tensor_copy(out=sctx[:], in_=Sc[:]), reads=[Sc.R], writes=[sctx.R])
            chain(0, list(range(NLT)), True, False)
            chain(1, list(range(NLT - 1, -1, -1)), True, False)
            K.dma(K.sp, self.EXS.rearrange("(a p) v -> p a v", p=128), Sc[:].rearrange("p d (h v) -> p (d h) v", h=4),
                  src=[Sc.R], dst=self.R_EXS)
            self.collective(self.EXS, self.R_EXS, self.EXSG, self.R_EXSG)
            K.dma(K.sp, TG[:, 0, :].rearrange("p (h v) -> p h v", h=4), self.EXSG[0:512, :].rearrange("(h p) v -> p h v", p=128),
                  src=[self.R_EXSG], dst=TG.R)
            K.dma(K.sp, TG[:, 1, :].rearrange("p (h v) -> p h v", h=4), self.EXSG[1536:2048, :].rearrange("(h p) v -> p h v", p=128),
                  src=[self.R_EXSG], dst=TG.R, partial=True)
            for d_ in range(2):
                for h in range(4):
                    i = d_ * 4 + h
                    V(lambda d_=d_, h=h, i=i: nc.vector.tensor_scalar(out=Sc[:, d_, HS[h]], in0=sctx[:, d_, HS[h]],
                                                                      scalar1=dec[:, 48 + i:49 + i], scalar2=None, op0=ALU.mult),
                      reads=[sctx.R, dec.R], writes=[Sc.R])
                    V(lambda d_=d_, h=h: nc.vector.scalar_tensor_tensor(out=Sc[:, d_, HS[h]], in0=TG[:, d_, HS[h]],
                                                                         scalar=self.flags[:, d_:d_ + 1], in1=Sc[:, d_, HS[h]],
                                                                         op0=ALU.mult, op1=ALU.add),
                      reads=[TG.R, self.flags.R, Sc.R], writes=[Sc.R])
            chain(0, list(range(NLT)), False, True)
            chain(1, list(range(NLT - 1, -1, -1)), False, True)

            qdf = dec[:, 40:44].unsqueeze(2).to_broadcast([128, 4, 128])
            qdb = dec[:, 44:48].unsqueeze(2).to_broadcast([128, 4, 128])
            v3 = lambda ap: ap.rearrange("p (h v) -> p h v", h=4)
            if stopx == f"B1c{l}":
                ychunks = []
            if stopx == f"B1d{l}":
                ychunks = ychunks[:1]
            for c in ychunks:
                cs_ = slice(c * 128, (c + 1) * 128)
                gt_ = Gs.next()
                K.dma(K.sp, gt_[:, :], self.SK[cs_, 1536:2048], src=[self.R_SK], dst=gt_.R)
                bS = self.P.next()
                self.mms(bS, [(bS[:, HS[h]], kT[:, h, cs_], qT[:, h, cs_]) for h in range(4)], reads=[kT.R, qT.R])
                pm = Pm.next()
                V(lambda bS=bS, pm=pm: nc.vector.tensor_tensor(out=v3(pm[:, :]), in0=v3(bS[:, :]), in1=M[:, :, :], op=ALU.mult),
                  reads=[bS.R, M.R], writes=[pm.R])
                bI = self.P.next()
                self.mms(bI, [(bI[:, HS[h]], pm[:, HS[h]], Vt[:, c, HS[h]]) for h in range(4)], reads=[pm.R, Vt.R])
                bF = self.P.next()
                self.mms(bF, [(bF[:, HS[h]], qT[:, h, cs_], Sbf[:, 0, c, HS[h]]) for h in range(4)], reads=[qT.R, Sbf.R])
                bB = self.P.next()
                self.mms(bB, [(bB[:, HS[h]], qT[:, h, cs_], Sbf[:, 1, c, HS[h]]) for h in range(4)], reads=[qT.R, Sbf.R])
                y = ysb.next()
                tb_ = t2.next()
                V(lambda bF=bF, y=y: nc.vector.tensor_tensor(out=v3(y[:, :]), in0=v3(bF[:, :]), in1=qdf, op=ALU.mult),
                  reads=[bF.R, dec.R], writes=[y.R])
                V(lambda bB=bB, tb_=tb_: nc.vector.tensor_tensor(out=v3(tb_[:, :]), in0=v3(bB[:, :]), in1=qdb, op=ALU.mult),
                  reads=[bB.R, dec.R], writes=[tb_.R])
                V(lambda bI=bI, y=y: nc.vector.tensor_tensor(out=y[:, :], in0=y[:, :], in1=bI[:, :], op=ALU.add),
                  reads=[bI.R, y.R], writes=[y.R])
                K.op(K.pool, lambda y=y, tb_=tb_: nc.gpsimd.tensor_tensor(out=y[:, :], in0=y[:, :], in1=tb_[:, :], op=ALU.add),
                     reads=[y.R, tb_.R], writes=[y.R])
                st_ = stt.next()
                V(lambda y=y, st_=st_: nc.vector.reduce_sum(out=st_[:, 0:4], in_=v3(y[:, :]), axis=AX.X), reads=[y.R], writes=[st_.R])
                K.op(K.pool, lambda y=y: nc.gpsimd.tensor_tensor(out=sq[:, :], in0=y[:, :], in1=y[:, :], op=ALU.mult),
                     reads=[y.R], writes=[sq.R])
                V(lambda st_=st_: nc.vector.reduce_sum(out=st_[:, 4:8], in_=v3(sq[:, :]), axis=AX.X), reads=[sq.R, st_.R], writes=[st_.R])
                V(lambda st_=st_: nc.vector.tensor_scalar(out=st_[:, 8:12], in0=st_[:, 0:4], scalar1=1.0 / 128, scalar2=None, op0=ALU.mult),
                  reads=[st_.R], writes=[st_.R])
                V(lambda st_=st_: nc.vector.tensor_tensor(out=st_[:, 12:16], in0=st_[:, 8:12], in1=st_[:, 8:12], op=ALU.mult),
                  reads=[st_.R], writes=[st_.R])
                V(lambda st_=st_: nc.vector.scalar_tensor_tensor(out=st_[:, 4:8], in0=st_[:, 4:8], scalar=1.0 / 128, in1=st_[:, 12:16],
                                                                  op0=ALU.mult, op1=ALU.subtract), reads=[st_.R], writes=[st_.R])
                A(lambda st_=st_: nc.scalar.activation(out=st_[:, 0:4], in_=st_[:, 4:8], func=AF.Sqrt, bias=self.epsc[:, 0:1], scale=1.0),
                  reads=[st_.R, self.epsc.R], writes=[st_.R])
                V(lambda st_=st_: nc.vector.reciprocal(out=st_[:, 4:8], in_=st_[:, 0:4]), reads=[st_.R], writes=[st_.R])
                V(lambda y=y, st_=st_: nc.vector.tensor_tensor(out=v3(y[:, :]), in0=v3(y[:, :]),
                                                               in1=st_[:, 8:12].unsqueeze(2).to_broadcast([128, 4, 128]), op=ALU.subtract),
                  reads=[y.R, st_.R], writes=[y.R])
                V(lambda y=y, st_=st_: nc.vector.tensor_tensor(out=v3(y[:, :]), in0=v3(y[:, :]),
                                                               in1=st_[:, 4:8].unsqueeze(2).to_broadcast([128, 4, 128]), op=ALU.mult),
                  reads=[y.R, st_.R], writes=[y.R])
                o = yo.next()
                V(lambda y=y, o=o, gt_=gt_: nc.vector.tensor_tensor(out=o[:, :], in0=y[:, :], in1=gt_[:, :], op=ALU.mult),
                  reads=[y.R, gt_.R], writes=[o.R])
                pt = self.PT.next()
                for h in range(4):
                    K.tr(pt[:, HS[h]], o[:, HS[h]], self.ident[:], reads=[o.R, self.ident.R],
                         writes=[pt.R] if h == 0 else (), pwrites=() if h == 0 else [pt.R])
                ys = ystg.next()
                A(lambda pt=pt, ys=ys: nc.scalar.copy(out=ys[:, :, :], in_=pt[:, 0:512].rearrange("p (b t) -> p b t", b=4)),
                  reads=[pt.R], writes=[ys.R])
                K.dma(K.sp, self.SY[0:512, cs_].rearrange("(b p) t -> p b t", p=128), ys[:, :, :], src=[ys.R], dst=self.R_SY, partial=True)
            allt = [Kt, Vt, KLb, kT, qT, Sbf, Sc, sctx, tmpS, TG, dec, mskt, mtmp, M, sq]
            rings = Qs.tiles + Gs.tiles + Pm.tiles + ysb.tiles + t2.tiles + stt.tiles + yo.tiles + ystg.tiles
            self.drain([t.R for t in allt + rings])

    def phase_B2(self, l):
        K, nc, I = self.K, self.nc, self.I
        first = (l == 0)
        with contextlib.ExitStack() as es2:
            U = Tile(K, "U", [128, 32, 512], BF16, es2)
            for q in range(4):
                K.dma(K.sp, U[:, q * 8:(q + 1) * 8, :], self.EXUG[q * 1024:(q + 1) * 1024, :].rearrange("(t p) c -> p t c", p=128),
                      src=[self.R_EXUG], dst=U.R, partial=(q > 0))
            C128 = self.const_load("C128", [128, 128], BF16, es2)
            nS128 = self.const_load("nS128", [128, 128], BF16, es2)
            slots = Ring([Tile(K, f"dft{i}", [128, 8, 512], BF16, es2) for i in range(6)])
            AB = Ring([Tile(K, f"AB{i}", [128, 512], BF16, es2) for i in range(4)])
            ystg = Ring([Tile(K, f"fy{i}", [128, 512], BF16, es2) for i in range(2)])
            TB = (I["CL"], I["SL"])
            plan = [(kb, gp, tbl, q) for kb in range(4) for gp in range(2) for tbl in range(2) for q in range(4)]
            loads = [(lambda slot, kb=kb, tbl=tbl, q=q: K.dma(
                K.sp, slot[:, :, :], TB[tbl][q * 1024:(q + 1) * 1024, kb * 512:(kb + 1) * 512].rearrange("(n p) k -> p n k", p=128),
                dst=slot.R)) for (kb, gp, tbl, q) in plan]
            st = self.WStream(self, loads, depth=3, ring=slots)

            def stage2(bA, bB, rows, cols, w):
                a_sb, b_sb = AB.next(), AB.next()
                K.op(K.act, lambda: nc.scalar.copy(out=a_sb[:, 0:w], in_=bA[:, 0:w]), reads=[bA.R], writes=[a_sb.R])
                K.op(K.dve, lambda: nc.vector.tensor_copy(out=b_sb[:, 0:w], in_=bB[:, 0:w]), reads=[bB.R], writes=[b_sb.R])
                bY = self.P.next()
                K.mmg(bY[:, 0:w], [(C128[:, :], a_sb[:, 0:w]), (nS128[:, :], b_sb[:, 0:w])],
                      reads=[C128.R, nS128.R, a_sb.R, b_sb.R], writes=[bY.R])
                ys = ystg.next()
                K.op(K.act, lambda: nc.scalar.copy(out=ys[:, 0:w], in_=bY[:, 0:w]), reads=[bY.R], writes=[ys.R])
                K.dma(K.sp, self.SY[rows, cols], ys[:, 0:w], src=[ys.R], dst=self.R_SY, partial=True)

            li = 0
            for kb in range(4):
                for gp in range(2):
                    banks = {(tbl, g): self.P.next() for tbl in range(2) for g in (2 * gp, 2 * gp + 1)}
                    for tbl in range(2):
                        for q in range(4):
                            slot = st.get(li)
                            li += 1
                            for g in (2 * gp, 2 * gp + 1):
                                bk = banks[(tbl, g)]
                                K.mmg(bk[:, :], [(U[:, q * 8 + n, g * 128:(g + 1) * 128], slot[:, n, :]) for n in range(8)],
                                      reads=[U.R, slot.R], writes=[bk.R], start=(q == 0), stop=(q == 3))
                    for g in (2 * gp, 2 * gp + 1):
                        stage2(banks[(0, g)], banks[(1, g)], slice(512 + g * 128, 512 + (g + 1) * 128),
                               slice(kb * 512, (kb + 1) * 512), 512)
            regs = [U.R, C128.R, nS128.R] + [t.R for t in slots.tiles + AB.tiles + ystg.tiles]
            if first:
                Uc = Tile(K, "Uc", [128, 2, 512], BF16, es2)
                K.dma(K.sp, Uc[:], self.SUC.rearrange("(t p) c -> p t c", p=128), src=[self.R_SUC], dst=Uc.R)
                C256 = self.const_load("C256", [128, 2, 256], BF16, es2, src=I["C256"].rearrange("(n p) k -> p n k", p=128))
                S256 = self.const_load("S256", [128, 2, 256], BF16, es2, src=I["S256"].rearrange("(n p) k -> p n k", p=128))
                for g in range(4):
                    bA, bB = self.P.next(), self.P.next()
                    K.mmg(bA[:, 0:256], [(Uc[:, n, g * 128:(g + 1) * 128], C256[:, n, :]) for n in range(2)],
                          reads=[Uc.R, C256.R], writes=[bA.R])
                    K.mmg(bB[:, 0:256], [(Uc[:, n, g * 128:(g + 1) * 128], S256[:, n, :]) for n in range(2)],
                          reads=[Uc.R, S256.R], writes=[bB.R])
                    stage2(bA, bB, slice(512 + g * 128, 512 + (g + 1) * 128), slice(LT, LT + CT), 256)
                regs += [Uc.R, C256.R, S256.R]
            self.drain(regs)

    def phase_B34(self, l):
        K, nc, I = self.K, self.nc, self.I
        first = (l == 0)
        A = lambda fn, **kw: K.op(K.act, fn, **kw)
        V = lambda fn, **kw: K.op(K.dve, fn, **kw)
        G = lambda fn, **kw: K.op(K.pool, fn, **kw)
        WMAX = LT + 32
        with contextlib.ExitStack() as es2:
            scw = self.const_load("sc_w", [128, 4, 3], F32, es2, src=I["sc_w"][l])
            cfw = self.const_load("cf_w", [128, 4, 31], F32, es2, src=I["cf_w"][l])
            cfln = self.const_load("cf_ln", [128, 4, 2], F32, es2, src=I["cf_ln"][l])
            HLf = Tile(K, "HLf", [128, 20, 32], BF16, es2)
            HRf = Tile(K, "HRf", [128, 20, 32], BF16, es2)
            HL = Tile(K, "HL", [128, 20, 16], BF16, es2)
            HR = Tile(K, "HR", [128, 20, 16], BF16, es2)
            K.dma(K.sp, HLf[:, :, :].rearrange("p a w -> p (a w)"), self.EXHG[0:128, :], src=[self.R_EXHG], dst=HLf.R)
            K.dma(K.sp, HRf[:, :, :].rearrange("p a w -> p (a w)"), self.EXHG[128:256, :], src=[self.R_EXHG], dst=HRf.R)
            V(lambda: nc.vector.tensor_scalar(out=HL[:], in0=HLf[:, :, 16:32], scalar1=self.flags[:, 0:1], scalar2=None, op0=ALU.mult),
              reads=[HLf.R, self.flags.R], writes=[HL.R])
            V(lambda: nc.vector.tensor_scalar(out=HR[:], in0=HRf[:, :, 0:16], scalar1=self.flags[:, 1:2], scalar2=None, op0=ALU.mult),
              reads=[HRf.R, self.flags.R], writes=[HR.R])
            E = []
            for i in range(2):
                t = Tile(K, f"E{i}", [128, WMAX], BF16, es2)
                t.RH = Reg(f"E{i}h")
                E.append(t)
            Mf = Tile(K, "Mf", [128, WMAX], F32, es2)
            accD = Tile(K, "accD", [128, LT], F32, es2)
            accP = Tile(K, "accP", [128, LT], F32, es2)
            Bt = Tile(K, "Bt", [128, LT], BF16, es2)
            yo = Tile(K, "cyo", [128, LT], BF16, es2)
            CV = Tile(K, "CV", [128, 4, LT], F32, es2)
            SQ = Tile(K, "SQ", [128, 4, 512], F32, es2)
            mt = Tile(K, "mt", [128, 3, 512], F32, es2)
            tt = Ring([Tile(K, f"ctt{i}", [128, 512], F32, es2) for i in range(2)])
            og = Ring([Tile(K, f"cog{i}", [128, 512], BF16, es2) for i in range(2)])
            segs = [(0, LT, True)] + ([(LT, CT, False)] if first else [])

            def load_ext(e, kind, blk, tok0, W, halo):
                a = kind * 4 + blk
                if halo:
                    V(lambda: nc.vector.tensor_copy(out=e[:, 0:16], in_=HL[:, a, :]), reads=[HL.R], writes=[e.RH])
                    V(lambda: nc.vector.tensor_copy(out=e[:, 16 + W:32 + W], in_=HR[:, a, :]), reads=[HR.R], pwrites=[e.RH])
                else:
                    V(lambda: nc.vector.memset(e[:, 0:16], 0.0), writes=[e.RH])
                    V(lambda: nc.vector.memset(e[:, 16 + W:32 + W], 0.0), pwrites=[e.RH])
                r0 = kind * 512 + blk * 128
                K.dma(K.sp, e[:, 16:16 + W], self.SF[r0:r0 + 128, tok0:tok0 + W], src=[self.R_SF], dst=e.R)

            for (tok0, W, halo) in segs:
                for blk in range(4):
                    load_ext(E[0], 1, blk, tok0, W, halo)
                    load_ext(E[1], 2, blk, tok0, W, halo)
                    K.dma(K.sp, Bt[:, 0:W], self.SF[blk * 128:(blk + 1) * 128, tok0:tok0 + W], src=[self.R_SF], dst=Bt.R)
                    V(lambda: nc.vector.tensor_tensor(out=Mf[:, 0:W + 32], in0=E[0][:, 0:W + 32], in1=E[1][:, 0:W + 32], op=ALU.mult),
                      reads=[E[0].R, E[0].RH, E[1].R, E[1].RH], writes=[Mf.R])
                    V(lambda blk=blk: nc.vector.tensor_scalar(out=accD[:, 0:W], in0=Mf[:, 15:15 + W], scalar1=scw[:, blk, 0:1],
                                                               scalar2=None, op0=ALU.mult), reads=[Mf.R, scw.R], writes=[accD.R])
                    for j in (1, 2):
                        V(lambda blk=blk, j=j: nc.vector.scalar_tensor_tensor(out=accD[:, 0:W], in0=Mf[:, 15 + j:15 + j + W],
                                                                               scalar=scw[:, blk, j:j + 1], in1=accD[:, 0:W],
                                                                               op0=ALU.mult, op1=ALU.add),
                          reads=[Mf.R, scw.R, accD.R], writes=[accD.R])
                    V(lambda: nc.vector.tensor_tensor(out=yo[:, 0:W], in0=accD[:, 0:W], in1=Bt[:, 0:W], op=ALU.mult),
                      reads=[accD.R, Bt.R], writes=[yo.R])
                    K.dma(K.sp, self.SY[1024 + blk * 128:1024 + (blk + 1) * 128, tok0:tok0 + W], yo[:, 0:W],
                          src=[yo.R], dst=self.R_SY, partial=True)
                for blk in range(4):
                    load_ext(E[0], 3, blk, tok0, W, halo)
                    load_ext(E[1], 4, blk, tok0, W, halo)
                    A(lambda: nc.scalar.activation(out=Mf[:, 0:W + 32], in_=E[1][:, 0:W + 32], func=AF.Sigmoid),
                      reads=[E[1].R, E[1].RH], writes=[Mf.R])
                    V(lambda: nc.vector.tensor_tensor(out=Mf[:, 0:W + 32], in0=Mf[:, 0:W + 32], in1=E[0][:, 0:W + 32], op=ALU.mult),
                      reads=[Mf.R, E[0].R, E[0].RH], writes=[Mf.R])
                    V(lambda blk=blk: nc.vector.tensor_scalar(out=accD[:, 0:W], in0=Mf[:, 1:1 + W], scalar1=cfw[:, blk, 0:1],
                                                               scalar2=None, op0=ALU.mult), reads=[Mf.R, cfw.R], writes=[accD.R])
                    for j in range(1, 31):
                        V(lambda blk=blk, j=j: nc.vector.scalar_tensor_tensor(out=accD[:, 0:W], in0=Mf[:, 1 + j:1 + j + W],
                                                                               scalar=cfw[:, blk, j:j + 1], in1=accD[:, 0:W],
                                                                               op0=ALU.mult, op1=ALU.add),
                          reads=[Mf.R, cfw.R, accD.R], writes=[accD.R])
                    A(lambda blk=blk: nc.scalar.copy(out=CV[:, blk, 0:W], in_=accD[:, 0:W]), reads=[accD.R], pwrites=[CV.R])
                for tb in range(0, W, 512):
                    w = min(512, W - tb)
                    A(lambda tb=tb, w=w: nc.scalar.activation(out=SQ[:, :, 0:w], in_=CV[:, :, tb:tb + w], func=AF.Square),
                      reads=[CV.R], writes=[SQ.R])
                    b1, b2 = self.P.next(), self.P.next()
                    K.mmg(b1[:, 0:w], [(self.ones_f[:, :], CV[:, blk, tb:tb + w]) for blk in range(4)],
                          reads=[self.ones_f.R, CV.R], writes=[b1.R])
                    K.mmg(b2[:, 0:w], [(self.ones_f[:, :], SQ[:, blk, 0:w]) for blk in range(4)],
                          reads=[self.ones_f.R, SQ.R], writes=[b2.R])
                    A(lambda b1=b1, w=w: nc.scalar.mul(out=mt[:, 0, 0:w], in_=b1[:, 0:w], mul=1.0 / 512), reads=[b1.R], writes=[mt.R])
                    V(lambda w=w: nc.vector.tensor_tensor(out=mt[:, 1, 0:w], in0=mt[:, 0, 0:w], in1=mt[:, 0, 0:w], op=ALU.mult),
                      reads=[mt.R], writes=[mt.R])
                    V(lambda b2=b2, w=w: nc.vector.scalar_tensor_tensor(out=mt[:, 2, 0:w], in0=b2[:, 0:w], scalar=1.0 / 512,
                                                                         in1=mt[:, 1, 0:w], op0=ALU.mult, op1=ALU.subtract),
                      reads=[b2.R, mt.R], writes=[mt.R])
                    A(lambda w=w: nc.scalar.activation(out=mt[:, 1, 0:w], in_=mt[:, 2, 0:w], func=AF.Sqrt, bias=self.epsc[:, 0:1], scale=1.0),
                      reads=[mt.R, self.epsc.R], writes=[mt.R])
                    V(lambda w=w: nc.vector.reciprocal(out=mt[:, 2, 0:w], in_=mt[:, 1, 0:w]), reads=[mt.R], writes=[mt.R])
                    for blk in range(4):
                        t_ = tt.next()
                        V(lambda blk=blk, t_=t_, tb=tb, w=w: nc.vector.tensor_tensor(out=t_[:, 0:w], in0=CV[:, blk, tb:tb + w],
                                                                                      in1=mt[:, 0, 0:w], op=ALU.subtract),
                          reads=[CV.R, mt.R], writes=[t_.R])
                        G(lambda t_=t_, w=w: nc.gpsimd.tensor_tensor(out=t_[:, 0:w], in0=t_[:, 0:w], in1=mt[:, 2, 0:w], op=ALU.mult),
                          reads=[t_.R, mt.R], writes=[t_.R])
                        o = og.next()
                        A(lambda blk=blk, t_=t_, o=o, w=w: nc.scalar.activation(out=o[:, 0:w], in_=t_[:, 0:w], func=AF.Silu,
                                                                                 scale=cfln[:, blk, 0:1], bias=cfln[:, blk, 1:2]),
                          reads=[t_.R, cfln.R], writes=[o.R])
                        K.dma(K.sp, self.SY[1536 + blk * 128:1536 + (blk + 1) * 128, tok0 + tb:tok0 + tb + w], o[:, 0:w],
                              src=[o.R], dst=self.R_SY, partial=True)
            regs = [scw.R, cfw.R, cfln.R, HL.R, HR.R, HLf.R, HRf.R, Mf.R, accD.R, accP.R, Bt.R, yo.R, CV.R, SQ.R, mt.R]
            regs += [E[0].R, E[0].RH, E[1].R, E[1].RH] + [t.R for t in tt.tiles + og.tiles]
            self.drain(regs)

    def phase_C(self, l):
        K, nc, I = self.K, self.nc, self.I
        first, last = (l == 0), (l == DEPTH - 1)
        A = lambda fn, **kw: K.op(K.act, fn, **kw)
        V = lambda fn, **kw: K.op(K.dve, fn, **kw)
        G = lambda fn, **kw: K.op(K.pool, fn, **kw)
        NTOK = 256
        with contextlib.ExitStack() as es2:
            wsr = self.alloc_ws(es2)
            ROW = [Tile(K, f"row{v}", [128, D], F32, es2) for v in range(2)]
            YB = Tile(K, "YB", [128, 16, NTOK], BF16, es2)
            SGt = Ring([Tile(K, f"SGt{i}", [128, 4, NTOK], BF16, es2) for i in range(2)])
            WB = Ring([Tile(K, f"WB{i}", [128, 16, 128], BF16, es2) for i in range(2)])
            merged = Tile(K, "merged", [128, 16, NTOK], BF16, es2)
            tm = [Tile(K, f"tm{i}", [128, NTOK], F32, es2) for i in range(4)]
            MIX = Tile(K, "MIX", [128, 2, D], F32, es2)
            xm = [Tile(K, f"xm{i}", [128, D], F32, es2) for i in range(2)]
            ssq = Tile(K, "ssq", [128, 2, 8, 8], F32, es2)
            ssr = Tile(K, "ssrC", [128, 4], F32, es2)
            xn2 = Tile(K, "xn2", [128, D], BF16, es2)
            h2T = Tile(K, "h2T", [128, 16, NTOK], BF16, es2)
            aT = Tile(K, "aT", [128, 64, NTOK], BF16, es2)
            rl = Ring([Tile(K, f"rl{i}", [128, NTOK], BF16, es2) for i in range(2)])
            rtmp_regs = []

            def wloads():
                ls = []
                for cbk in range(4):
                    ls.append(lambda slot, cbk=cbk: self.wload(slot, I["w_out"][l, :, cbk * 512:(cbk + 1) * 512]))
                for fb in range(16):
                    ls.append(lambda slot, fb=fb: self.wload(slot, I["w_ff1"][l, :, fb * 512:(fb + 1) * 512]))
                for cbk in range(4):
                    for q in range(4):
                        ls.append(lambda slot, cbk=cbk, q=q: self.wload(
                            slot, I["w_ff2"][l, q * 2048:(q + 1) * 2048, cbk * 512:(cbk + 1) * 512]))
                return ls

            def rms_update(t, vrow):
                V(lambda: nc.vector.reduce_sum(out=ssq[:, t, 4, 0:1], in_=ssq[:, t, 0:4, 0], axis=AX.X), reads=[ssq.R], writes=[ssq.R])
                A(lambda: nc.scalar.activation(out=ssq[:, t, 5, 0:1], in_=ssq[:, t, 4, 0:1], func=AF.Sqrt, scale=1.0 / D,
                                               bias=self.epsc[:, 0:1]), reads=[ssq.R, self.epsc.R], writes=[ssq.R])
                V(lambda: nc.vector.reciprocal(out=ssq[:, t, 6, 0:1], in_=ssq[:, t, 5, 0:1]), reads=[ssq.R], writes=[ssq.R])
                V(lambda: nc.vector.scalar_tensor_tensor(out=MIX[:, t, :], in0=MIX[:, t, :], scalar=ssq[:, t, 6, 0:1],
                                                         in1=ROW[vrow][:, :], op0=ALU.mult, op1=ALU.mult),
                  reads=[MIX.R, ssq.R, ROW[vrow].R], writes=[MIX.R])
                V(lambda: nc.vector.tensor_tensor(out=xm[t][:, :], in0=xm[t][:, :], in1=MIX[:, t, :], op=ALU.add),
                  reads=[MIX.R, xm[t].R], writes=[xm[t].R])

            def evac_tok(bank, t, cbk):
                dstv = MIX[:, t, cbk * 512:(cbk + 1) * 512]
                V(lambda: nc.vector.tensor_copy(out=dstv, in_=bank[:, :]), reads=[bank.R], writes=[MIX.R])
                A(lambda: nc.scalar.activation(out=self.junk[:, 0:512], in_=dstv, func=AF.Square,
                                               accum_out=ssq[:, t, cbk, 0:1]), reads=[MIX.R], writes=[self.junk.R, ssq.R])

            def do_blocks(blocks, kind):
                loads = []
                for _ in blocks:
                    loads += wloads()
                st = self.WStream(self, loads)
                li = 0
                for t0 in blocks:
                    tok0 = t0 * 128
                    for t in range(2):
                        src, Rs = self.x_src(l, t0 + t)
                        K.dma(K.sp, xm[t][:, :], src, src=[Rs] if Rs else [], dst=xm[t].R)
                    K.dma(K.sp, YB[:, :, :], self.SY[:, tok0:tok0 + NTOK].rearrange("(a p) t -> p a t", p=128),
                          src=[self.R_SY], dst=YB.R)
                    if self.cstop == "C1":
                        return
                    for dblk in range(16):
                        wb = WB.next()
                        K.dma(K.pool, wb[:, :, :], I["w_branch"][l][:, :, dblk * 128:(dblk + 1) * 128].rearrange(
                            "n (wc p) c -> p (n wc) c", p=128), dst=wb.R)
                        sg = SGt.next()
                        K.dma(K.sp, sg[:, :, :], self.SG.rearrange("(n r) t -> r n t", n=4)[dblk * 128:(dblk + 1) * 128, :, tok0:tok0 + NTOK],
                              src=[self.R_SG], dst=sg.R)
                        for n in range(4):
                            bank = self.P.next()
                            K.mmg(bank[:, 0:NTOK], [(wb[:, n * 4 + wc, :], YB[:, n * 4 + wc, :]) for wc in range(4)],
                                  reads=[wb.R, YB.R], writes=[bank.R])
                            V(lambda bank=bank, n=n, sg=sg: nc.vector.tensor_tensor(out=tm[n][:, :], in0=bank[:, 0:NTOK], in1=sg[:, n, :],
                                                                                     op=ALU.mult), reads=[bank.R, sg.R], writes=[tm[n].R])
                        G(lambda: nc.gpsimd.tensor_tensor(out=tm[0][:, :], in0=tm[0][:, :], in1=tm[1][:, :], op=ALU.add),
                          reads=[tm[0].R, tm[1].R], writes=[tm[0].R])
                        G(lambda: nc.gpsimd.tensor_tensor(out=tm[2][:, :], in0=tm[2][:, :], in1=tm[3][:, :], op=ALU.add),
                          reads=[tm[2].R, tm[3].R], writes=[tm[2].R])
                        G(lambda dblk=dblk: nc.gpsimd.tensor_tensor(out=merged[:, dblk, :], in0=tm[0][:, :], in1=tm[2][:, :], op=ALU.add),
                          reads=[tm[0].R, tm[2].R], pwrites=[merged.R])
                    if self.cstop == "C2":
                        return
                    for cbk in range(4):
                        w = st.get(li)
                        li += 1
                        for t in range(2):
                            bank = self.P.next()
                            K.mmg(bank[:, :], [(merged[:, dc, t * 128:(t + 1) * 128], w[:, dc, :]) for dc in range(16)],
                                  reads=[merged.R, w.R], writes=[bank.R])
                            evac_tok(bank, t, cbk)
                    if self.cstop == "C2b":
                        return
                    for t in range(2):
                        rms_update(t, 0)
                    if self.cstop == "C3":
                        return
                    for t in range(2):
                        self.norm_mod_T(xm[t], 1, 3, kind, h2T, t * 128, xn2, ssr)
                    if self.cstop == "C4":
                        return
                    for fb in range(16):
                        w = st.get(li)
                        li += 1
                        for j in range(4):
                            bank = self.P.next()
                            K.mmg(bank[:, 0:NTOK], [(w[:, kc, j * 128:(j + 1) * 128], h2T[:, kc, :]) for kc in range(16)],
                                  reads=[w.R, h2T.R], writes=[bank.R])
                            r = rl.next()
                            A(lambda bank=bank, r=r: nc.scalar.activation(out=r[:, :], in_=bank[:, 0:NTOK], func=AF.Relu),
                              reads=[bank.R], writes=[r.R])
                            V(lambda r=r, fb=fb, j=j: nc.vector.tensor_tensor(out=aT[:, fb * 4 + j, :], in0=r[:, :], in1=r[:, :], op=ALU.mult),
                              reads=[r.R], pwrites=[aT.R])
                    if self.cstop == "C5":
                        return
                    for cbk in range(4):
                        banks = [self.P.next() for _ in range(2)]
                        for q in range(4):
                            w = st.get(li)
                            li += 1
                            for t in range(2):
                                K.mmg(banks[t][:, :], [(aT[:, q * 16 + fc, t * 128:(t + 1) * 128], w[:, fc, :]) for fc in range(16)],
                                      reads=[aT.R, w.R], writes=[banks[t].R], start=(q == 0), stop=(q == 3))
                        for t in range(2):
                            evac_tok(banks[t], t, cbk)
                    if self.cstop == "C6":
                        return
                    for t in range(2):
                        rms_update(t, 1)
                        gt = t0 + t
                        if last:
                            K.dma(K.sp, self.out[gt * 128:(gt + 1) * 128, :], xm[t][:, :], src=[xm[t].R], dst=self.R_out, partial=True)
                        else:
                            K.dma(K.sp, self.XS[gt * 128:(gt + 1) * 128, :], xm[t][:, :], src=[xm[t].R], dst=self.R_XS, partial=True)

            if first:
                rtmp_regs += self.phase_Prow(l, 1, ROW, es2)
                do_blocks([NLT], 1)
            if self.cstop is None or self.cstop.startswith("L"):
                rtmp_regs += self.phase_Prow(l, 0, ROW, es2)
                do_blocks([2 * i for i in range(8 if self.cstop is None else int(self.cstop[1:]))], 0)
            regs = wsr + rtmp_regs + [t.R for t in ROW + [YB, merged, MIX, ssq, ssr, xn2, h2T, aT] + tm + xm + SGt.tiles + WB.tiles + rl.tiles]
            self.drain(regs)


_CONST_CACHE = {}


def _prep_inputs(x, c, ctx, c_ctx, w_ada, b_ada, norm_g, w_in, ret_decay, sc_conv, cf_conv, cf_ln,
                 w_branch, w_out, w_ff1, w_ff2):
    f = lambda a: np.ascontiguousarray(np.asarray(a), dtype=np.float32)
    x, c, ctx, c_ctx = f(x), f(c), f(ctx), f(c_ctx)
    shared = {
        "w_ada": f(w_ada), "w_in": f(w_in), "w_branch": f(w_branch), "w_out": f(w_out),
        "w_ff1": f(w_ff1), "w_ff2": f(w_ff2),
        "b_col": np.ascontiguousarray(f(b_ada).reshape(DEPTH, 96, 128).transpose(0, 2, 1)),
        "b_row": f(b_ada).reshape(DEPTH, 1, 6 * D),
        "ng_col": np.ascontiguousarray(f(norm_g).reshape(DEPTH, 4, 16, 128).transpose(0, 3, 1, 2)),
        "ng_row": f(norm_g),
        "ret_decay": np.ascontiguousarray(np.broadcast_to(f(ret_decay).reshape(DEPTH, 1, 8), (DEPTH, 128, 8))),
        "sc_w": np.ascontiguousarray(f(sc_conv).reshape(DEPTH, 3, 4, 128).transpose(0, 3, 2, 1)),
        "cf_w": np.ascontiguousarray(f(cf_conv).reshape(DEPTH, 31, 4, 128).transpose(0, 3, 2, 1)),
        "cf_ln": np.ascontiguousarray(f(cf_ln).reshape(DEPTH, 2, 4, 128).transpose(0, 3, 2, 1)),
    }
    for hf in range(2):
        if hf not in _CONST_CACHE:
            _CONST_CACHE[hf] = _host_consts(hf)
    in_maps = []
    for core in range(NCORES):
        b, hf = core // 2, core % 2
        m = dict(shared)
        m.update(_CONST_CACHE[hf])
        m["x_in"] = np.ascontiguousarray(x[b, hf * LT:(hf + 1) * LT, :])
        m["ctx_in"] = np.ascontiguousarray(ctx[b])
        cv = np.stack([c[b], c_ctx], -1).reshape(16, 128, 2).transpose(1, 0, 2)
        m["cvec"] = np.ascontiguousarray(cv)
        in_maps.append(m)
    return in_maps


_NC_CACHE = {}


def kernel(**inputs):
    in_maps = _prep_inputs(**inputs)
    if "nc" not in _NC_CACHE:
        _NC_CACHE["nc"] = Prog().build()
    res = run_bass_kernel_spmd(_NC_CACHE["nc"], in_maps, core_ids=list(range(NCORES)))
    out = np.empty((4, 2 * LT, D), np.float32)
    for core in range(NCORES):
        b, hf = core // 2, core % 2
        out[b, hf * LT:(hf + 1) * LT, :] = res.results[core]["out"]
    return out
```

```python
import contextlib
import math
import numpy as np
import ml_dtypes
import concourse.bass as bass
import concourse.mybir as mybir
from concourse.bass_utils import run_bass_kernel_spmd

F32 = mybir.dt.float32
BF16 = mybir.dt.bfloat16
AF = mybir.ActivationFunctionType
ALU = mybir.AluOpType
AX = mybir.AxisListType

D = 2048
DEPTH = 2
LT = 2048
CT = 256
T = LT + CT
NT = T // 128
NLT = LT // 128
BW = 512
INC = 13312
DFF = 8192
EPS = 1e-6
NCORES = 8
DEBUG = False


class Ev:
    __slots__ = ("sem", "val", "key")

    def __init__(self, sem, val, key):
        self.sem, self.val, self.key = sem, val, key

    def resolve(self):
        return self


class LazyEv:
    __slots__ = ("reg",)

    def __init__(self, reg):
        self.reg = reg

    def resolve(self):
        if self.reg.dsem is None:
            return None
        return Ev(self.reg.dsem, self.reg.dcnt, id(self.reg.dsem))


class Reg:
    def __init__(self, name, dram=False):
        self.name = name
        self.dram = dram
        self.w = {}
        self.r = {}
        self.dsem = None
        self.dcnt = 0
        self.dq = None


class Eng:
    def __init__(self, K, name, h):
        self.K, self.name, self.h = K, name, h
        self.sem = None
        self.cnt = 0
        self.seen = {}
        self.nsem = 0

    def wait(self, ev):
        if ev is None:
            return
        ev = ev.resolve()
        if ev is None:
            return
        if self.seen.get(ev.key, 0) >= ev.val:
            return
        self.h.wait_ge(ev.sem, ev.val)
        self.seen[ev.key] = ev.val

    def emit(self, ins):
        if self.sem is None or self.cnt >= 30000:
            self.sem = self.K.newsem(f"e_{self.name}_{self.nsem}")
            self.nsem += 1
            self.cnt = 0
        self.cnt += 1
        ins.then_inc(self.sem, 1)
        return Ev(self.sem, self.cnt, id(self.sem))


class KB:
    def __init__(self, nc, es):
        self.nc, self.es = nc, es
        self.pe = Eng(self, "pe", nc.tensor)
        self.act = Eng(self, "act", nc.scalar)
        self.dve = Eng(self, "dve", nc.vector)
        self.pool = Eng(self, "pool", nc.gpsimd)
        self.sp = Eng(self, "sp", nc.sync)
        self.nsems = 0
        self.same_engine_wait = True
        self.uid = 0
        self.free_sems = {}

    def newsem(self, name):
        self.nsems += 1
        return self.es.enter_context(self.nc.semaphore(f"{name}_{self.nsems}"))

    def sb(self, name, shape, dt, es=None):
        self.uid += 1
        return (es or self.es).enter_context(self.nc.sbuf_tensor(f"{name}_{self.uid}", list(shape), dt))

    def ps(self, name, shape, dt):
        return self.es.enter_context(self.nc.psum_tensor(name, list(shape), dt))

    def _pre(self, eng, reads, writes, pwrites):
        for r in reads:
            for ev in r.w.values():
                eng.wait(ev)
        for w in writes:
            for ev in w.w.values():
                eng.wait(ev)
            for ev in w.r.values():
                eng.wait(ev)
        for w in pwrites:
            for ev in w.r.values():
                eng.wait(ev)

    def _post(self, ev, reads, writes, pwrites):
        for r in reads:
            r.r[ev.key] = ev
        for w in writes:
            w.w = {ev.key: ev}
            w.r = {}
        for w in pwrites:
            w.w = {ev.key: ev}
            w.r = {}

    def op(self, eng, fn, reads=(), writes=(), pwrites=()):
        self._pre(eng, reads, writes, pwrites)
        ins = fn()
        ev = eng.emit(ins)
        if not self.same_engine_wait or eng is self.pe:
            eng.seen[ev.key] = ev.val
        self._post(ev, reads, writes, pwrites)
        return ev

    def mmg(self, out, pairs, reads=(), writes=(), start=True, stop=True):
        pe = self.pe
        self._pre(pe, reads, writes if start else (), ())
        n = len(pairs)
        ins = None
        for i, (l, r) in enumerate(pairs):
            ins = self.nc.tensor.matmul(out, l, r, start=(start and i == 0), stop=(stop and i == n - 1))
        ev = pe.emit(ins)
        pe.seen[ev.key] = ev.val
        self._post(ev, reads, writes, ())
        return ev

    def tr(self, out, in_, ident, reads=(), writes=(), pwrites=()):
        return self.op(self.pe, lambda: self.nc.tensor.transpose(out, in_, ident),
                       reads=reads, writes=writes, pwrites=pwrites)

    def dma(self, q, out, in_, src=(), dst=None, partial=False, owner=None):
        store = dst.dram and len(src) > 0 and not src[0].dram
        if owner is None:
            owner = src[0] if store else dst
            assert not owner.dram, (owner.name, dst.name)
        for r in src:
            for ev in r.w.values():
                q.wait(ev)
        for ev in dst.r.values():
            q.wait(ev)
        for ev in dst.w.values():
            if not (partial and isinstance(ev, LazyEv)):
                q.wait(ev)
        if owner.dsem is None:
            fl = self.free_sems.setdefault(q.name, [])
            if fl:
                owner.dsem, owner.dcnt = fl.pop()
            else:
                owner.dsem = self.newsem("d_" + owner.name)
            owner.dq = q.name
        assert owner.dq == q.name, (owner.name, owner.dq, q.name)
        ins = q.h.dma_start(out=out, in_=in_)
        owner.dcnt += 16
        ins.then_inc(owner.dsem, 16)
        lev = LazyEv(owner)
        for r in src:
            r.r[id(owner)] = lev
        if partial:
            dst.w[id(owner)] = lev
        else:
            dst.w = {id(owner): lev}
        dst.r = {}
        return lev


class Tile:
    def __init__(self, K, name, shape, dt, es=None, psum=False):
        self.t = K.ps(name, shape, dt) if psum else K.sb(name, shape, dt, es)
        self.R = Reg(name)

    def __getitem__(self, k):
        return self.t[k]


class Ring:
    def __init__(self, tiles):
        self.tiles = tiles
        self.i = 0

    def next(self):
        t = self.tiles[self.i % len(self.tiles)]
        self.i += 1
        return t


def _bf(a):
    return np.ascontiguousarray(a).astype(ml_dtypes.bfloat16)


def _host_consts(hf):
    c = {}
    c["ident"] = _bf(np.eye(128, dtype=np.float32))
    c["ones_f"] = np.ones((128, 128), np.float32)
    tok = hf * LT + np.arange(LT)
    row = (tok // 64).astype(np.float32)
    col = (tok % 64).astype(np.float32)
    freqs = (np.float32(10000.0) ** (-np.arange(32, dtype=np.float32) / np.float32(32))).astype(np.float32)
    ang32 = np.concatenate([row[:, None] * freqs[None], col[:, None] * freqs[None]], -1).astype(np.float32)
    cs = np.stack([np.cos(ang32.astype(np.float64)), np.sin(ang32.astype(np.float64))], 1)
    cs = cs.reshape(NLT, 128, 2, 64).transpose(1, 0, 2, 3)
    c["ropeq"] = np.ascontiguousarray(cs).astype(np.float32)
    c["ropek"] = np.ascontiguousarray(cs * (128.0 ** -0.5)).astype(np.float32)
    m = np.arange(128)[:, None].astype(np.float32)
    n = np.arange(128)[None, :].astype(np.float32)
    msk = np.stack([np.maximum(n - m, 0), np.maximum(m - n, 0), (n >= m).astype(np.float32),
                    (m > n).astype(np.float32)], 1)
    c["msk"] = np.ascontiguousarray(msk).astype(np.float32)
    p = np.arange(128, dtype=np.float32)
    c["pcol"] = np.stack([127 - p, p, p + 1, 128 - p], 1).astype(np.float32)
    c["flags"] = np.tile(np.array([[hf, 1 - hf]], np.float32), (128, 1))
    L = 4096
    kk = hf * LT + np.arange(LT, dtype=np.int64)
    nn = np.arange(L, dtype=np.int64)
    r = (nn[:, None] * kk[None, :]) % L
    th = 2.0 * np.pi * r.astype(np.float64) / L
    sc = 1.0 / math.sqrt(L * 128.0)
    c["CL"] = _bf(np.cos(th) * sc)
    c["SL"] = _bf(np.sin(th) * sc)
    j = np.arange(128, dtype=np.int64)
    th2 = 2.0 * np.pi * ((j[:, None] * j[None, :]) % 128).astype(np.float64) / 128
    c["C128"] = _bf(np.cos(th2))
    c["nS128"] = _bf(-np.sin(th2))
    q = np.arange(256, dtype=np.int64)
    th3 = 2.0 * np.pi * ((q[:, None] * q[None, :]) % 256).astype(np.float64) / 256
    sc3 = 1.0 / math.sqrt(256 * 128.0)
    c["C256"] = _bf(np.cos(th3) * sc3)
    c["S256"] = _bf(np.sin(th3) * sc3)
    return c


CONST_SPECS = {
    "ident": ([128, 128], BF16), "ones_f": ([128, 128], F32),
    "ropeq": ([128, 16, 2, 64], F32), "ropek": ([128, 16, 2, 64], F32),
    "msk": ([128, 4, 128], F32), "pcol": ([128, 4], F32), "flags": ([128, 2], F32),
    "CL": ([4096, 2048], BF16), "SL": ([4096, 2048], BF16),
    "C128": ([128, 128], BF16), "nS128": ([128, 128], BF16),
    "C256": ([256, 256], BF16), "S256": ([256, 256], BF16),
}

IN_SPECS = {
    "x_in": ([LT, D], F32), "ctx_in": ([CT, D], F32), "cvec": ([128, 16, 2], F32),
    "w_ada": ([DEPTH, D, 6 * D], F32), "b_col": ([DEPTH, 128, 96], F32), "b_row": ([DEPTH, 1, 6 * D], F32),
    "ng_col": ([DEPTH, 128, 4, 16], F32), "ng_row": ([DEPTH, 4, D], F32),
    "w_in": ([DEPTH, D, INC], F32), "ret_decay": ([DEPTH, 128, 8], F32),
    "sc_w": ([DEPTH, 128, 4, 3], F32), "cf_w": ([DEPTH, 128, 4, 31], F32), "cf_ln": ([DEPTH, 128, 4, 2], F32),
    "w_branch": ([DEPTH, 4, BW, D], F32), "w_out": ([DEPTH, D, D], F32),
    "w_ff1": ([DEPTH, D, DFF], F32), "w_ff2": ([DEPTH, DFF, D], F32),
}


class Prog:
    def __init__(self, debug=False, stop_after=None, nlayers=DEPTH, ncores=NCORES, nocoll=False):
        self.debug, self.stop_after, self.nlayers = debug, stop_after, nlayers
        self.nocoll = nocoll
        self.skipB = False
        self.cstop = None
        self.groups = [[2 * i, 2 * i + 1] for i in range(ncores // 2)]
        self.nc = bass.Bass("TRN2", target_bir_lowering=False)
        self.es = contextlib.ExitStack()

    def dram(self, name, shape, dt, dbg=True):
        kind = "ExternalOutput" if (self.debug and dbg) else "Internal"
        return self.nc.dram_tensor(name, list(shape), dt, kind=kind).ap()

    def build(self):
        nc = self.nc
        with self.es as es:
            self.K = K = KB(nc, es)
            self.I = I = {}
            for name, (shape, dt) in {**IN_SPECS, **CONST_SPECS}.items():
                I[name] = nc.dram_tensor(name, list(shape), dt, kind="ExternalInput").ap()
            self.out = nc.dram_tensor("out", [LT, D], F32, kind="ExternalOutput").ap()
            self.R_out = Reg("out", dram=True)
            self.XS = self.dram("XS", [T, D], F32); self.R_XS = Reg("XS", dram=True)
            self.SK = self.dram("SK", [T, 2048], BF16); self.R_SK = Reg("SK", dram=True)
            self.SUC = self.dram("SUC", [CT, 512], BF16); self.R_SUC = Reg("SUC", dram=True)
            self.SF = self.dram("SF", [2560, T], BF16); self.R_SF = Reg("SF", dram=True)
            self.SG = self.dram("SG", [8192, T], BF16); self.R_SG = Reg("SG", dram=True)
            self.SY = self.dram("SY", [2048, T], BF16); self.R_SY = Reg("SY", dram=True)
            self.EXU = self.dram("EXU", [LT, 512], BF16, dbg=False); self.R_EXU = Reg("EXU", dram=True)
            self.EXUG = self.dram("EXUG", [2 * LT, 512], BF16, dbg=False); self.R_EXUG = Reg("EXUG", dram=True)
            self.EXS = self.dram("EXS", [1024, 128], F32, dbg=False); self.R_EXS = Reg("EXS", dram=True)
            self.EXSG = self.dram("EXSG", [2048, 128], F32, dbg=False); self.R_EXSG = Reg("EXSG", dram=True)
            self.EXH = self.dram("EXH", [128, 640], BF16, dbg=False); self.R_EXH = Reg("EXH", dram=True)
            self.EXHG = self.dram("EXHG", [256, 640], BF16, dbg=False); self.R_EXHG = Reg("EXHG", dram=True)
            self.WC = self.nc.dram_tensor("WC", [62, 128, 8192], BF16, kind="Internal").ap()
            self.R_WC = [Reg(f"WC{i}", dram=True) for i in range(62)]
            self.R_st = [Reg(f"wst{i}") for i in range(4)]
            self.wc_valid = {}
            self.wc_n = 0
            self.cc_sem = K.newsem("cc")
            self.ccdummy = Tile(K, "ccdummy", [128, 4], F32)
            self.junk = Tile(K, "junk", [128, D], BF16)
            self.cc_cnt = 0
            self.P = Ring([Tile(K, f"P{i}", [128, 512], F32, psum=True) for i in range(6)])
            self.PT = Ring([Tile(K, f"PT{i}", [128, 1024], BF16, psum=True) for i in range(2)])
            self.ident = self.const_load("ident", [128, 128], BF16)
            self.ones_f = self.const_load("ones_f", [128, 128], F32)
            self.flags = self.const_load("flags", [128, 2], F32)
            self.pcol = self.const_load("pcol", [128, 4], F32)
            self.epsc = Tile(K, "epsc", [128, 1], F32)
            K.op(K.dve, lambda: nc.vector.memset(self.epsc[:], EPS), writes=[self.epsc.R])
            self.cs = Tile(K, "cs", [128, 16, 2], BF16)
            cv = self.const_load("cvec", [128, 16, 2], F32)
            K.op(K.act, lambda: nc.scalar.activation(out=self.cs[:], in_=cv[:], func=AF.Silu),
                 reads=[cv.R], writes=[self.cs.R])
            self.modc = Tile(K, "modc", [128, 6, 16, 2], F32)
            self.MA = Tile(K, "MA", [128, 2, 16, 2], F32)

            for l in range(self.nlayers):
                self.phase_P(l)
                if self.stop_after == f"P{l}":
                    break
                self.phase_A(l)
                if self.stop_after == f"A{l}":
                    break
                if not self.skipB:
                    self.phase_B(l)
                if self.stop_after and self.stop_after.startswith("B") and self.stop_after.endswith(str(l)):
                    break
                self.phase_C(l)
                if self.stop_after == f"C{l}":
                    break
            self.finish()
        return nc

    def const_load(self, name, shape, dt, es=None, src=None):
        K = self.K
        t = Tile(K, name, shape, dt, es)
        K.dma(K.sp, t[:], self.I[name] if src is None else src, dst=t.R)
        return t

    def finish(self):
        K = self.K
        for R in (self.R_out, self.R_XS, self.R_SK, self.R_SUC, self.R_SF, self.R_SG, self.R_SY,
                  self.R_EXU, self.R_EXS, self.R_EXH, self.R_EXUG, self.R_EXSG, self.R_EXHG, *self.R_WC):
            for ev in list(R.w.values()) + list(R.r.values()):
                K.sp.wait(ev)
        for e in (K.pe, K.act, K.dve, K.pool):
            if e.sem is not None:
                K.sp.wait(Ev(e.sem, e.cnt, id(e.sem)))

    def collective(self, src_ap, R_src, dst_ap, R_dst):
        K, nc = self.K, self.nc
        pool = K.pool
        if self.nocoll:
            rows = src_ap.shape[0]
            pp = min(rows, 128)
            for half in range(2):
                K.dma(K.sp, dst_ap[half * rows:(half + 1) * rows, :].rearrange("(p r) c -> p (r c)", p=pp),
                      src_ap.rearrange("(p r) c -> p (r c)", p=pp), src=[R_src], dst=self.ccdummy.R, partial=(half == 1))
            for ev in self.ccdummy.R.w.values():
                K.sp.wait(ev)
            ev = K.op(K.dve, lambda: nc.vector.memset(self.ccdummy[:], 0.0), writes=[self.ccdummy.R])
            R_src.r[ev.key] = ev
            R_dst.w = {ev.key: ev}
            R_dst.r = {}
            return
        K._pre(pool, [R_src], [R_dst], [])
        ins = nc.gpsimd.collective_compute(
            "AllGather", ALU.bypass, replica_groups=self.groups,
            ins=[src_ap.opt()], outs=[dst_ap.opt()])
        self.cc_cnt += 1
        ins.then_inc(self.cc_sem, 1)
        nc.gpsimd.wait_ge(self.cc_sem, self.cc_cnt)
        ev = pool.emit(nc.gpsimd.memset(self.ccdummy[:], 0.0))
        R_src.r[ev.key] = ev
        R_dst.w = {ev.key: ev}
        R_dst.r = {}

    class WStream:
        def __init__(self, prog, loads, depth=2, ring=None):
            self.prog, self.loads, self.depth = prog, loads, depth
            self.ring = ring if ring is not None else prog.WS
            self.issued = 0
            self.slots = {}

        def get(self, i):
            while self.issued < min(len(self.loads), i + self.depth + 1):
                slot = self.ring.next()
                self.loads[self.issued](slot)
                self.slots[self.issued] = slot
                self.issued += 1
            return self.slots.pop(i)

    def wtile(self, slot, l, e, src_ap):
        K = self.K
        wcv = self.WC[e].rearrange("p (k c) -> p k c", k=16)
        if self.wc_valid.get(e) != l:
            self.wload(slot, src_ap)
            ow = self.R_st[self.wc_n % 4]
            self.wc_n += 1
            K.dma(K.sp, wcv, slot[:, :, :], src=[slot.R], dst=self.R_WC[e], owner=ow)
            self.wc_valid[e] = l
        else:
            K.dma(K.pool, slot[:, :, :], wcv, src=[self.R_WC[e]], dst=slot.R)

    def wload(self, slot, src_ap, nk=16, ncol=512):
        K = self.K
        K.dma(K.pool, slot[:, 0:nk, 0:ncol], src_ap.rearrange("(k p) c -> p k c", p=128), dst=slot.R)

    def alloc_ws(self, es2, n=3):
        self.WS = Ring([Tile(self.K, f"ws{i}", [128, 16, 512], BF16, es2) for i in range(n)])
        return [t.R for t in self.WS.tiles]

    def phase_P(self, l):
        K, nc, I = self.K, self.nc, self.I
        with contextlib.ExitStack() as es2:
            wsr = self.alloc_ws(es2)
            bcol = self.const_load("b_col", [128, 96], F32, es2, src=I["b_col"][l])
            ngcol = self.const_load("ng_col", [128, 4, 16], F32, es2, src=I["ng_col"][l])
            cbs = [cb for cb in range(24) if cb // 4 in (0, 1, 3, 4)]
            loads = [(lambda slot, cb=cb: self.wload(slot, I["w_ada"][l, :, cb * 512:(cb + 1) * 512])) for cb in cbs]
            st = self.WStream(self, loads)
            cs = self.cs
            for i, cb in enumerate(cbs):
                wa = st.get(i)
                v, j4 = cb // 4, cb % 4
                for j in range(4):
                    bank = self.P.next()
                    K.mmg(bank[:, 0:2], [(wa[:, kc, j * 128:(j + 1) * 128], cs[:, kc, :]) for kc in range(16)],
                          reads=[wa.R, cs.R], writes=[bank.R])
                    idx = v * 16 + j4 * 4 + j
                    K.op(K.dve, lambda bank=bank, idx=idx, v=v, dc=j4 * 4 + j: nc.vector.tensor_scalar(
                        out=self.modc[:, v, dc, :], in0=bank[:, 0:2], scalar1=bcol[:, idx:idx + 1],
                        scalar2=None, op0=ALU.add), reads=[bank.R, bcol.R], pwrites=[self.modc.R])
            for a, (vsc, gi) in enumerate(((1, 0), (4, 2))):
                K.op(K.dve, lambda a=a, vsc=vsc, gi=gi: nc.vector.scalar_tensor_tensor(
                    out=self.MA[:, a, :, :], in0=self.modc[:, vsc, :, :], scalar=1.0,
                    in1=ngcol[:, gi, :].unsqueeze(2).to_broadcast([128, 16, 2]),
                    op0=ALU.add, op1=ALU.mult), reads=[self.modc.R, ngcol.R], pwrites=[self.MA.R])
            self.drain([bcol.R, ngcol.R] + wsr)

    def phase_Prow(self, l, r, ROW, es2):
        K, nc, I = self.K, self.nc, self.I
        rtmp = Ring([Tile(K, f"rtmp{i}", [1, 3, 512], F32, es2) for i in range(2)])
        cbs = [cb for cb in range(24) if cb // 4 in (2, 5)]
        loads = [(lambda slot, cb=cb: self.wload(slot, I["w_ada"][l, :, cb * 512:(cb + 1) * 512])) for cb in cbs]
        st = self.WStream(self, loads)
        cs = self.cs
        for i, cb in enumerate(cbs):
            wa = st.get(i)
            v, j4 = cb // 4, cb % 4
            vi = 0 if v == 2 else 1
            rt = rtmp.next()
            K.dma(K.sp, rt[0:1, 1, :], I["b_row"][l, :, v * D + j4 * 512: v * D + (j4 + 1) * 512], dst=rt.R)
            K.dma(K.sp, rt[0:1, 2, :], I["ng_row"][l, 1 + 2 * vi:2 + 2 * vi, j4 * 512:(j4 + 1) * 512],
                  dst=rt.R, partial=True)
            bank = self.P.next()
            K.mmg(bank[0:1, :], [(cs[:, kc, r:r + 1], wa[:, kc, :]) for kc in range(16)],
                  reads=[wa.R, cs.R], writes=[bank.R])
            K.op(K.dve, lambda bank=bank, rt=rt: nc.vector.tensor_tensor(
                out=rt[0:1, 0, :], in0=bank[0:1, :], in1=rt[0:1, 1, :], op=ALU.add),
                reads=[bank.R, rt.R], writes=[rt.R])
            K.op(K.dve, lambda rt=rt: nc.vector.tensor_tensor(
                out=rt[0:1, 0, :], in0=rt[0:1, 0, :], in1=rt[0:1, 2, :], op=ALU.mult),
                reads=[rt.R], writes=[rt.R])
            bank2 = self.P.next()
            K.mmg(bank2[:, :], [(self.ones_f[0:1, :], rt[0:1, 0, :])],
                  reads=[self.ones_f.R, rt.R], writes=[bank2.R])
            row = ROW[vi]
            K.op(K.act, lambda bank2=bank2, row=row, j4=j4: nc.scalar.copy(
                out=row[:, j4 * 512:(j4 + 1) * 512], in_=bank2[:, :]),
                reads=[bank2.R], pwrites=[row.R])
        return [t.R for t in rtmp.tiles]

    def drain(self, regs):
        K = self.K
        evs = []
        for R in regs:
            evs.extend(R.w.values())
            evs.extend(R.r.values())
        for e in (K.pe, K.act, K.dve, K.pool, K.sp):
            for ev in evs:
                e.wait(ev)
        for R in regs:
            if R.dsem is not None:
                if R.dcnt < 40000:
                    K.free_sems.setdefault(R.dq, []).append((R.dsem, R.dcnt))
                R.dsem = None
            R.w = {}
            R.r = {}

    def norm_mod_T(self, xt, a_idx, s_idx, kind, hT, toff, xn, ssr, eng_sq=None):
        K, nc = self.K, self.nc
        junk = self.junk
        K.op(K.act, lambda: nc.scalar.activation(out=junk[:], in_=xt[:], func=AF.Square, accum_out=ssr[:, 0:1]),
             reads=[xt.R], writes=[junk.R, ssr.R])
        K.op(K.act, lambda: nc.scalar.activation(out=ssr[:, 1:2], in_=ssr[:, 0:1], func=AF.Sqrt, scale=1.0 / D,
                                                 bias=self.epsc[:, 0:1]), reads=[ssr.R, self.epsc.R], writes=[ssr.R])
        K.op(K.dve, lambda: nc.vector.reciprocal(out=ssr[:, 2:3], in_=ssr[:, 1:2]), reads=[ssr.R], writes=[ssr.R])
        K.op(K.dve, lambda: nc.vector.tensor_scalar(out=xn[:], in0=xt[:], scalar1=ssr[:, 2:3], scalar2=None,
                                                    op0=ALU.mult), reads=[ssr.R, xt.R], writes=[xn.R])
        for half in range(2):
            pt = self.PT.next()
            for j in range(8):
                dc = half * 8 + j
                K.tr(pt[:, j * 128:(j + 1) * 128], xn[:, dc * 128:(dc + 1) * 128], self.ident[:],
                     reads=[xn.R, self.ident.R], writes=[pt.R] if j == 0 else (), pwrites=() if j == 0 else [pt.R])
            ptv = pt[:].rearrange("p (a b) -> p a b", a=8)
            dst = hT[:, half * 8:(half + 1) * 8, toff:toff + 128]
            K.op(K.dve, lambda ptv=ptv, dst=dst, half=half: nc.vector.tensor_tensor(
                out=dst, in0=ptv, in1=self.MA[:, a_idx, half * 8:(half + 1) * 8, kind:kind + 1].to_broadcast([128, 8, 128]),
                op=ALU.mult), reads=[pt.R, self.MA.R], pwrites=[hT.R])
            K.op(K.dve, lambda dst=dst, half=half: nc.vector.tensor_tensor(
                out=dst, in0=dst, in1=self.modc[:, s_idx, half * 8:(half + 1) * 8, kind:kind + 1].to_broadcast([128, 8, 128]),
                op=ALU.add), reads=[self.modc.R, hT.R], pwrites=[hT.R])

    def x_src(self, l, gt):
        if l == 0:
            if gt < NLT:
                return self.I["x_in"][gt * 128:(gt + 1) * 128, :], None
            return self.I["ctx_in"][(gt - NLT) * 128:(gt - NLT + 1) * 128, :], None
        return self.XS[gt * 128:(gt + 1) * 128, :], self.R_XS

    def rope(self, bank, tab, gt, out, rtmp):
        K, nc = self.K, self.nc
        psv = bank[:, :].rearrange("p (h a j) -> p h a j", h=4, a=2)
        ov = out[:, :].rearrange("p (h a j) -> p h a j", h=4, a=2)
        t1, t2 = psv[:, :, 0, :], psv[:, :, 1, :]
        cosb = tab[:, gt, 0:1, :].to_broadcast([128, 4, 64])
        sinb = tab[:, gt, 1:2, :].to_broadcast([128, 4, 64])
        for i, (a, b) in enumerate(((t1, cosb), (t2, sinb), (t1, sinb), (t2, cosb))):
            K.op(K.dve, lambda a=a, b=b, i=i: nc.vector.tensor_tensor(out=rtmp[:, i, :, :], in0=a, in1=b, op=ALU.mult),
                 reads=[bank.R, tab.R], writes=[rtmp.R] if i == 0 else (), pwrites=() if i == 0 else [rtmp.R])
        K.op(K.dve, lambda: nc.vector.tensor_tensor(out=ov[:, :, 0, :], in0=rtmp[:, 0, :, :], in1=rtmp[:, 1, :, :],
                                                    op=ALU.subtract), reads=[rtmp.R], writes=[out.R])
        K.op(K.dve, lambda: nc.vector.tensor_tensor(out=ov[:, :, 1, :], in0=rtmp[:, 2, :, :], in1=rtmp[:, 3, :, :],
                                                    op=ALU.add), reads=[rtmp.R], pwrites=[out.R])

    def phase_A(self, l):
        K, nc, I = self.K, self.nc, self.I
        last = (l == DEPTH - 1)
        blocks = [(tb * 4, 4) for tb in range(4)] + [(NLT, 2)]
        with contextlib.ExitStack() as es2:
            ropeq = self.const_load("ropeq", [128, 16, 2, 64], F32, es2)
            ropek = self.const_load("ropek", [128, 16, 2, 64], F32, es2)
            xts = Ring([Tile(K, f"xt{i}", [128, D], F32, es2) for i in range(2)])
            xn = Tile(K, "xn", [128, D], BF16, es2)
            ssr = Tile(K, "ssr", [128, 4], F32, es2)
            hT = Tile(K, "hT", [128, 16, 512], BF16, es2)
            rtmp = Tile(K, "ropetmp", [128, 4, 4, 64], F32, es2)
            stg = Ring([Tile(K, f"stg{i}", [128, 512], BF16, es2) for i in range(4)])
            wsr = self.alloc_ws(es2)
            HB = Tile(K, "HB", [128, 20, 32], BF16, es2)
            loads = []
            plan = []
            for bi, (t0, nt) in enumerate(blocks):
                isctx = t0 >= NLT
                cbs = list(range(26)) if not (isctx and last) else [0, 1]
                for cb in cbs:
                    plan.append((bi, cb))
                    loads.append(lambda slot, cb=cb: self.wtile(slot, l, cb, I["w_in"][l, :, cb * 512:(cb + 1) * 512]))
            st = self.WStream(self, loads)
            li = 0
            cur_bi = -1
            for (bi, cb) in plan:
                t0, nt = blocks[bi]
                ntok = nt * 128
                isctx = t0 >= NLT
                kind = 1 if isctx else 0
                if bi != cur_bi:
                    cur_bi = bi
                    for t in range(nt):
                        gt = t0 + t
                        xt = xts.next()
                        src, Rs = self.x_src(l, gt)
                        K.dma(K.sp, xt[:], src, src=[Rs] if Rs else [], dst=xt.R)
                        self.norm_mod_T(xt, 0, 0, kind, hT, t * 128, xn, ssr)
                w = st.get(li)
                li += 1
                if cb < 5:
                    for t in range(nt):
                        gt = t0 + t
                        bank = self.P.next()
                        K.mmg(bank[:, :], [(hT[:, kc, t * 128:(t + 1) * 128], w[:, kc, :]) for kc in range(16)],
                              reads=[hT.R, w.R], writes=[bank.R])
                        o = stg.next()
                        if cb in (0, 2) and not isctx:
                            self.rope(bank, ropek if cb == 0 else ropeq, gt, o, rtmp)
                        elif cb == 0:
                            K.op(K.act, lambda bank=bank, o=o: nc.scalar.mul(out=o[:, :], in_=bank[:, :], mul=128.0 ** -0.5),
                                 reads=[bank.R], writes=[o.R])
                        elif cb == 3:
                            K.op(K.act, lambda bank=bank, o=o: nc.scalar.activation(out=o[:, :], in_=bank[:, :], func=AF.Silu),
                                 reads=[bank.R], writes=[o.R])
                        else:
                            K.op(K.act, lambda bank=bank, o=o: nc.scalar.copy(out=o[:, :], in_=bank[:, :]),
                                 reads=[bank.R], writes=[o.R])
                        if cb < 4:
                            K.dma(K.sp, self.SK[gt * 128:(gt + 1) * 128, cb * 512:(cb + 1) * 512], o[:, :],
                                  src=[o.R], dst=self.R_SK, partial=True)
                        elif not isctx:
                            K.dma(K.sp, self.EXU[gt * 128:(gt + 1) * 128, :], o[:, :], src=[o.R], dst=self.R_EXU, partial=True)
                        else:
                            K.dma(K.sp, self.SUC[(gt - NLT) * 128:(gt - NLT + 1) * 128, :], o[:, :],
                                  src=[o.R], dst=self.R_SUC, partial=True)
                else:
                    for j in range(4):
                        bank = self.P.next()
                        K.mmg(bank[:, 0:ntok], [(w[:, kc, j * 128:(j + 1) * 128], hT[:, kc, 0:ntok]) for kc in range(16)],
                              reads=[hT.R, w.R], writes=[bank.R])
                        o = stg.next()
                        fn = AF.Copy if cb < 10 else AF.Sigmoid
                        K.op(K.act, lambda bank=bank, o=o, fn=fn: nc.scalar.activation(out=o[:, 0:ntok], in_=bank[:, 0:ntok], func=fn),
                             reads=[bank.R], writes=[o.R])
                        if cb < 10:
                            r0 = (cb - 5) * 512 + j * 128
                            K.dma(K.sp, self.SF[r0:r0 + 128, t0 * 128:t0 * 128 + ntok], o[:, 0:ntok],
                                  src=[o.R], dst=self.R_SF, partial=True)
                            a = (cb - 5) * 4 + j
                            if t0 == 0:
                                K.op(K.act, lambda o=o, a=a: nc.scalar.copy(out=HB[:, a, 0:16], in_=o[:, 0:16]),
                                     reads=[o.R], pwrites=[HB.R])
                            if t0 == NLT - 4:
                                K.op(K.act, lambda o=o, a=a: nc.scalar.copy(out=HB[:, a, 16:32], in_=o[:, 496:512]),
                                     reads=[o.R], pwrites=[HB.R])
                        else:
                            r0 = (cb - 10) * 512 + j * 128
                            K.dma(K.sp, self.SG[r0:r0 + 128, t0 * 128:t0 * 128 + ntok], o[:, 0:ntok],
                                  src=[o.R], dst=self.R_SG, partial=True)
            K.dma(K.sp, self.EXH, HB[:, :, :].rearrange("p a w -> p (a w)"), src=[HB.R], dst=self.R_EXH)
            self.drain([t.R for t in (ropeq, ropek, xn, ssr, hT, rtmp, HB)] + [t.R for t in xts.tiles + stg.tiles] + wsr)

    def mms(self, bank, triples, reads):
        K, nc = self.K, self.nc
        pe = K.pe
        K._pre(pe, reads, [bank.R], ())
        ins = None
        for (o, lt, r) in triples:
            ins = nc.tensor.matmul(o, lt, r, start=True, stop=True)
        ev = pe.emit(ins)
        pe.seen[ev.key] = ev.val
        K._post(ev, reads, [bank.R], ())
        return ev

    def phase_B(self, l):
        self.collective(self.EXU, self.R_EXU, self.EXUG, self.R_EXUG)
        self.collective(self.EXH, self.R_EXH, self.EXHG, self.R_EXHG)
        self.phase_B1(l)
        if self.stop_after == f"B1{l}":
            return
        self.phase_B2(l)
        if self.stop_after == f"B2{l}":
            return
        self.phase_B34(l)

    def phase_B1(self, l):
        K, nc, I = self.K, self.nc, self.I
        first = (l == 0)
        ychunks = list(range(NT)) if first else list(range(NLT))
        HS = [slice(h * 128, (h + 1) * 128) for h in range(4)]
        with contextlib.ExitStack() as es2:
            Kt = Tile(K, "Kt", [128, NT, 512], BF16, es2)
            Vt = Tile(K, "Vt", [128, NT, 512], BF16, es2)
            KLb = Tile(K, "KLb", [128, NT, 512], BF16, es2)
            kT = Tile(K, "kT", [128, 4, T], BF16, es2)
            qT = Tile(K, "qT", [128, 4, T], BF16, es2)
            Sbf = Tile(K, "Sbf", [128, 2, NT, 512], BF16, es2)
            Sc = Tile(K, "Sc", [128, 2, 512], F32, es2)
            sctx = Tile(K, "sctx", [128, 2, 512], F32, es2)
            tmpS = Tile(K, "tmpS", [128, 512], F32, es2)
            TG = Tile(K, "TG", [128, 2, 512], F32, es2)
            dec = Tile(K, "dec", [128, 64], F32, es2)
            mskt = self.const_load("msk", [128, 4, 128], F32, es2)
            mtmp = Tile(K, "mtmp", [128, 2, 128], F32, es2)
            M = Tile(K, "M", [128, 4, 128], BF16, es2)
            Qs = Ring([Tile(K, f"Qs{i}", [128, 2, 512], BF16, es2) for i in range(2)])
            Gs = Ring([Tile(K, f"Gs{i}", [128, 512], BF16, es2) for i in range(2)])
            Pm = Ring([Tile(K, f"Pm{i}", [128, 512], BF16, es2) for i in range(2)])
            ysb = Ring([Tile(K, f"ysb{i}", [128, 512], F32, es2) for i in range(2)])
            t2 = Ring([Tile(K, f"t2{i}", [128, 512], F32, es2) for i in range(2)])
            sq = Tile(K, "sq", [128, 512], F32, es2)
            stt = Ring([Tile(K, f"stt{i}", [128, 16], F32, es2) for i in range(2)])
            yo = Ring([Tile(K, f"yo{i}", [128, 512], BF16, es2) for i in range(2)])
            ystg = Ring([Tile(K, f"ystg{i}", [128, 4, 128], BF16, es2) for i in range(2)])

            K.dma(K.sp, Kt[:], self.SK[:, 0:512].rearrange("(t p) c -> p t c", p=128), src=[self.R_SK], dst=Kt.R)
            K.dma(K.sp, Vt[:], self.SK[:, 512:1024].rearrange("(t p) c -> p t c", p=128), src=[self.R_SK], dst=Vt.R)
            K.dma(K.sp, dec[:, 0:8], I["ret_decay"][l], dst=dec.R)
            A = lambda fn, **kw: K.op(K.act, fn, **kw)
            V = lambda fn, **kw: K.op(K.dve, fn, **kw)
            A(lambda: nc.scalar.activation(out=dec[:, 8:16], in_=dec[:, 0:8], func=AF.Exp, scale=-1.0), reads=[dec.R], writes=[dec.R])
            A(lambda: nc.scalar.activation(out=dec[:, 8:16], in_=dec[:, 8:16], func=AF.Ln, bias=1.0, scale=1.0), reads=[dec.R], writes=[dec.R])
            V(lambda: nc.vector.tensor_scalar(out=dec[:, 16:24], in0=dec[:, 8:16], scalar1=-1.0, scalar2=None, op0=ALU.mult), reads=[dec.R], writes=[dec.R])
            A(lambda: nc.scalar.activation(out=dec[:, 24:32], in_=dec[:, 16:24], func=AF.Exp, scale=128.0), reads=[dec.R], writes=[dec.R])
            for d_ in range(2):
                for h in range(4):
                    i = d_ * 4 + h
                    A(lambda i=i, d_=d_: nc.scalar.activation(out=dec[:, 32 + i:33 + i], in_=self.pcol[:, d_:d_ + 1], func=AF.Exp,
                                                               scale=dec[:, 16 + i:17 + i]), reads=[dec.R, self.pcol.R], writes=[dec.R])
                    A(lambda i=i, d_=d_: nc.scalar.activation(out=dec[:, 40 + i:41 + i], in_=self.pcol[:, 2 + d_:3 + d_], func=AF.Exp,
                                                               scale=dec[:, 16 + i:17 + i]), reads=[dec.R, self.pcol.R], writes=[dec.R])
            for d_ in range(2):
                V(lambda d_=d_: nc.vector.tensor_scalar(out=dec[:, 56 + 4 * d_:60 + 4 * d_], in0=dec[:, 16 + 4 * d_:20 + 4 * d_],
                                                         scalar1=self.flags[:, d_:d_ + 1], scalar2=None, op0=ALU.mult),
                  reads=[dec.R, self.flags.R], writes=[dec.R])
            A(lambda: nc.scalar.activation(out=dec[:, 48:56], in_=dec[:, 56:64], func=AF.Exp, scale=2048.0), reads=[dec.R], writes=[dec.R])
            for h in range(4):
                A(lambda h=h: nc.scalar.activation(out=mtmp[:, 0, :], in_=mskt[:, 0, :], func=AF.Exp, scale=dec[:, 16 + h:17 + h]),
                  reads=[dec.R, mskt.R], writes=[mtmp.R])
                V(lambda: nc.vector.tensor_tensor(out=mtmp[:, 0, :], in0=mtmp[:, 0, :], in1=mskt[:, 2, :], op=ALU.mult), reads=[mtmp.R, mskt.R], writes=[mtmp.R])
                A(lambda h=h: nc.scalar.activation(out=mtmp[:, 1, :], in_=mskt[:, 1, :], func=AF.Exp, scale=dec[:, 20 + h:21 + h]),
                  reads=[dec.R, mskt.R, mtmp.R], writes=[mtmp.R])
                V(lambda: nc.vector.tensor_tensor(out=mtmp[:, 1, :], in0=mtmp[:, 1, :], in1=mskt[:, 3, :], op=ALU.mult), reads=[mtmp.R, mskt.R], writes=[mtmp.R])
                V(lambda h=h: nc.vector.tensor_tensor(out=M[:, h, :], in0=mtmp[:, 0, :], in1=mtmp[:, 1, :], op=ALU.add), reads=[mtmp.R], writes=[M.R])

            stopx = self.stop_after if (self.stop_after or "").startswith("B1") else None
            npairs = (NT if first else NLT) // 2
            if stopx == f"B1a{l}":
                npairs = 0
            for pi in range(npairs):
                g0 = 2 * pi
                qs = Qs.next()
                K.dma(K.sp, qs[:], self.SK[g0 * 128:(g0 + 2) * 128, 1024:1536].rearrange("(t p) c -> p t c", p=128),
                      src=[self.R_SK], dst=qs.R)
                for (srcT, srcR, dstT) in ((None, Kt.R, kT), (qs, qs.R, qT)):
                    pt = self.PT.next()
                    for tt in range(2):
                        for h in range(4):
                            sl = slice((h * 2 + tt) * 128, (h * 2 + tt + 1) * 128)
                            in_ = Kt[:, g0 + tt, HS[h]] if srcT is None else srcT[:, tt, HS[h]]
                            firstw = (tt == 0 and h == 0)
                            K.tr(pt[:, sl], in_, self.ident[:], reads=[srcR, self.ident.R],
                                 writes=[pt.R] if firstw else (), pwrites=() if firstw else [pt.R])
                    K.op(K.act, lambda pt=pt, dstT=dstT, g0=g0: nc.scalar.copy(
                        out=dstT[:, :, g0 * 128:(g0 + 2) * 128].rearrange("p h (t n) -> p h t n", t=2),
                        in_=pt[:].rearrange("p (h t n) -> p h t n", h=4, t=2)), reads=[pt.R], pwrites=[dstT.R])
            for h in range(4 if stopx != f"B1a{l}" else 0):
                V(lambda h=h: nc.vector.tensor_scalar(out=KLb[:, :, HS[h]], in0=Kt[:, :, HS[h]], scalar1=dec[:, 36 + h:37 + h],
                                                       scalar2=None, op0=ALU.mult), reads=[Kt.R, dec.R], pwrites=[KLb.R])
            for h in range(4 if stopx != f"B1a{l}" else 0):
                V(lambda h=h: nc.vector.tensor_scalar(out=Kt[:, :, HS[h]], in0=Kt[:, :, HS[h]], scalar1=dec[:, 32 + h:33 + h],
                                                       scalar2=None, op0=ALU.mult), reads=[Kt.R, dec.R], writes=[Kt.R])
            KL = (Kt, KLb)

            def chain(d_, chunks, zero_init, store):
                if zero_init:
                    V(lambda: nc.vector.memset(Sc[:, d_, :], 0.0), pwrites=[Sc.R])
                for c in chunks:
                    if store:
                        A(lambda c=c: nc.scalar.copy(out=Sbf[:, d_, c, :], in_=Sc[:, d_, :]), reads=[Sc.R], pwrites=[Sbf.R])
                    bank = self.P.next()
                    self.mms(bank, [(bank[:, HS[h]], KL[d_][:, c, HS[h]], Vt[:, c, HS[h]]) for h in range(4)],
                             reads=[KL[d_].R, Vt.R])
                    V(lambda: nc.vector.tensor_tensor(
                        out=tmpS[:, :].rearrange("p (h v) -> p h v", h=4), in0=Sc[:, d_, :].rearrange("p (h v) -> p h v", h=4),
                        in1=dec[:, 24 + 4 * d_:28 + 4 * d_].unsqueeze(2).to_broadcast([128, 4, 128]), op=ALU.mult),
                      reads=[Sc.R, dec.R], writes=[tmpS.R])
                    V(lambda bank=bank: nc.vector.tensor_tensor(out=Sc[:, d_, :], in0=tmpS[:, :], in1=bank[:, :], op=ALU.add),
                      reads=[tmpS.R, bank.R], writes=[Sc.R])

            if stopx in (f"B1a{l}", f"B1b{l}"):
                ychunks = []
            chain(0, [NLT, NLT + 1], True, first)
            chain(1, [NLT + 1, NLT], True, first)
            V(lambda: nc.vector.tensor_copy(out=sctx[:], in_=Sc[:]), reads=[Sc.R], writes=[sctx.R])
            chain(0, list(range(NLT)), True, False)
            chain(1, list(range(NLT - 1, -1, -1)), True, False)
            K.dma(K.sp, self.EXS.rearrange("(a p) v -> p a v", p=128), Sc[:].rearrange("p d (h v) -> p (d h) v", h=4),
                  src=[Sc.R], dst=self.R_EXS)
            self.collective(self.EXS, self.R_EXS, self.EXSG, self.R_EXSG)
            K.dma(K.sp, TG[:, 0, :].rearrange("p (h v) -> p h v", h=4), self.EXSG[0:512, :].rearrange("(h p) v -> p h v", p=128),
                  src=[self.R_EXSG], dst=TG.R)
            K.dma(K.sp, TG[:, 1, :].rearrange("p (h v) -> p h v", h=4), self.EXSG[1536:2048, :].rearrange("(h p) v -> p h v", p=128),
                  src=[self.R_EXSG], dst=TG.R, partial=True)
            for d_ in range(2):
                for h in range(4):
                    i = d_ * 4 + h
                    V(lambda d_=d_, h=h, i=i: nc.vector.tensor_scalar(out=Sc[:, d_, HS[h]], in0=sctx[:, d_, HS[h]],
                                                                      scalar1=dec[:, 48 + i:49 + i], scalar2=None, op0=ALU.mult),
                      reads=[sctx.R, dec.R], writes=[Sc.R])
                    V(lambda d_=d_, h=h: nc.vector.scalar_tensor_tensor(out=Sc[:, d_, HS[h]], in0=TG[:, d_, HS[h]],
                                                                         scalar=self.flags[:, d_:d_ + 1], in1=Sc[:, d_, HS[h]],
                                                                         op0=ALU.mult, op1=ALU.add),
                      reads=[TG.R, self.flags.R, Sc.R], writes=[Sc.R])
            chain(0, list(range(NLT)), False, True)
            chain(1, list(range(NLT - 1, -1, -1)), False, True)

            qdf = dec[:, 40:44].unsqueeze(2).to_broadcast([128, 4, 128])
            qdb = dec[:, 44:48].unsqueeze(2).to_broadcast([128, 4, 128])
            v3 = lambda ap: ap.rearrange("p (h v) -> p h v", h=4)
            if stopx == f"B1c{l}":
                ychunks = []
            if stopx == f"B1d{l}":
                ychunks = ychunks[:1]
            for c in ychunks:
                cs_ = slice(c * 128, (c + 1) * 128)
                gt_ = Gs.next()
                K.dma(K.sp, gt_[:, :], self.SK[cs_, 1536:2048], src=[self.R_SK], dst=gt_.R)
                bS = self.P.next()
                self.mms(bS, [(bS[:, HS[h]], kT[:, h, cs_], qT[:, h, cs_]) for h in range(4)], reads=[kT.R, qT.R])
                pm = Pm.next()
                V(lambda bS=bS, pm=pm: nc.vector.tensor_tensor(out=v3(pm[:, :]), in0=v3(bS[:, :]), in1=M[:, :, :], op=ALU.mult),
                  reads=[bS.R, M.R], writes=[pm.R])
                bI = self.P.next()
                self.mms(bI, [(bI[:, HS[h]], pm[:, HS[h]], Vt[:, c, HS[h]]) for h in range(4)], reads=[pm.R, Vt.R])
                bF = self.P.next()
                self.mms(bF, [(bF[:, HS[h]], qT[:, h, cs_], Sbf[:, 0, c, HS[h]]) for h in range(4)], reads=[qT.R, Sbf.R])
                bB = self.P.next()
                self.mms(bB, [(bB[:, HS[h]], qT[:, h, cs_], Sbf[:, 1, c, HS[h]]) for h in range(4)], reads=[qT.R, Sbf.R])
                y = ysb.next()
                tb_ = t2.next()
                V(lambda bF=bF, y=y: nc.vector.tensor_tensor(out=v3(y[:, :]), in0=v3(bF[:, :]), in1=qdf, op=ALU.mult),
                  reads=[bF.R, dec.R], writes=[y.R])
                V(lambda bB=bB, tb_=tb_: nc.vector.tensor_tensor(out=v3(tb_[:, :]), in0=v3(bB[:, :]), in1=qdb, op=ALU.mult),
                  reads=[bB.R, dec.R], writes=[tb_.R])
                V(lambda bI=bI, y=y: nc.vector.tensor_tensor(out=y[:, :], in0=y[:, :], in1=bI[:, :], op=ALU.add),
                  reads=[bI.R, y.R], writes=[y.R])
                K.op(K.pool, lambda y=y, tb_=tb_: nc.gpsimd.tensor_tensor(out=y[:, :], in0=y[:, :], in1=tb_[:, :], op=ALU.add),
                     reads=[y.R, tb_.R], writes=[y.R])
                st_ = stt.next()
                V(lambda y=y, st_=st_: nc.vector.reduce_sum(out=st_[:, 0:4], in_=v3(y[:, :]), axis=AX.X), reads=[y.R], writes=[st_.R])
                K.op(K.pool, lambda y=y: nc.gpsimd.tensor_tensor(out=sq[:, :], in0=y[:, :], in1=y[:, :], op=ALU.mult),
                     reads=[y.R], writes=[sq.R])
                V(lambda st_=st_: nc.vector.reduce_sum(out=st_[:, 4:8], in_=v3(sq[:, :]), axis=AX.X), reads=[sq.R, st_.R], writes=[st_.R])
                V(lambda st_=st_: nc.vector.tensor_scalar(out=st_[:, 8:12], in0=st_[:, 0:4], scalar1=1.0 / 128, scalar2=None, op0=ALU.mult),
                  reads=[st_.R], writes=[st_.R])
                V(lambda st_=st_: nc.vector.tensor_tensor(out=st_[:, 12:16], in0=st_[:, 8:12], in1=st_[:, 8:12], op=ALU.mult),
                  reads=[st_.R], writes=[st_.R])
                V(lambda st_=st_: nc.vector.scalar_tensor_tensor(out=st_[:, 4:8], in0=st_[:, 4:8], scalar=1.0 / 128, in1=st_[:, 12:16],
                                                                  op0=ALU.mult, op1=ALU.subtract), reads=[st_.R], writes=[st_.R])
                A(lambda st_=st_: nc.scalar.activation(out=st_[:, 0:4], in_=st_[:, 4:8], func=AF.Sqrt, bias=self.epsc[:, 0:1], scale=1.0),
                  reads=[st_.R, self.epsc.R], writes=[st_.R])
                V(lambda st_=st_: nc.vector.reciprocal(out=st_[:, 4:8], in_=st_[:, 0:4]), reads=[st_.R], writes=[st_.R])
                V(lambda y=y, st_=st_: nc.vector.tensor_tensor(out=v3(y[:, :]), in0=v3(y[:, :]),
                                                               in1=st_[:, 8:12].unsqueeze(2).to_broadcast([128, 4, 128]), op=ALU.subtract),
                  reads=[y.R, st_.R], writes=[y.R])
                V(lambda y=y, st_=st_: nc.vector.tensor_tensor(out=v3(y[:, :]), in0=v3(y[:, :]),
                                                               in1=st_[:, 4:8].unsqueeze(2).to_broadcast([128, 4, 128]), op=ALU.mult),
                  reads=[y.R, st_.R], writes=[y.R])
                o = yo.next()
                V(lambda y=y, o=o, gt_=gt_: nc.vector.tensor_tensor(out=o[:, :], in0=y[:, :], in1=gt_[:, :], op=ALU.mult),
                  reads=[y.R, gt_.R], writes=[o.R])
                pt = self.PT.next()
                for h in range(4):
                    K.tr(pt[:, HS[h]], o[:, HS[h]], self.ident[:], reads=[o.R, self.ident.R],
                         writes=[pt.R] if h == 0 else (), pwrites=() if h == 0 else [pt.R])
                ys = ystg.next()
                A(lambda pt=pt, ys=ys: nc.scalar.copy(out=ys[:, :, :], in_=pt[:, 0:512].rearrange("p (b t) -> p b t", b=4)),
                  reads=[pt.R], writes=[ys.R])
                K.dma(K.sp, self.SY[0:512, cs_].rearrange("(b p) t -> p b t", p=128), ys[:, :, :], src=[ys.R], dst=self.R_SY, partial=True)
            allt = [Kt, Vt, KLb, kT, qT, Sbf, Sc, sctx, tmpS, TG, dec, mskt, mtmp, M, sq]
            rings = Qs.tiles + Gs.tiles + Pm.tiles + ysb.tiles + t2.tiles + stt.tiles + yo.tiles + ystg.tiles
            self.drain([t.R for t in allt + rings])

    def phase_B2(self, l):
        K, nc, I = self.K, self.nc, self.I
        first = (l == 0)
        with contextlib.ExitStack() as es2:
            U = Tile(K, "U", [128, 32, 512], BF16, es2)
            for q in range(4):
                K.dma(K.sp, U[:, q * 8:(q + 1) * 8, :], self.EXUG[q * 1024:(q + 1) * 1024, :].rearrange("(t p) c -> p t c", p=128),
                      src=[self.R_EXUG], dst=U.R, partial=(q > 0))
            C128 = self.const_load("C128", [128, 128], BF16, es2)
            nS128 = self.const_load("nS128", [128, 128], BF16, es2)
            slots = Ring([Tile(K, f"dft{i}", [128, 8, 512], BF16, es2) for i in range(6)])
            AB = Ring([Tile(K, f"AB{i}", [128, 512], BF16, es2) for i in range(4)])
            ystg = Ring([Tile(K, f"fy{i}", [128, 512], BF16, es2) for i in range(2)])
            TB = (I["CL"], I["SL"])
            plan = [(kb, gp, tbl, q) for kb in range(4) for gp in range(2) for tbl in range(2) for q in range(4)]
            loads = [(lambda slot, kb=kb, tbl=tbl, q=q: K.dma(
                K.sp, slot[:, :, :], TB[tbl][q * 1024:(q + 1) * 1024, kb * 512:(kb + 1) * 512].rearrange("(n p) k -> p n k", p=128),
                dst=slot.R)) for (kb, gp, tbl, q) in plan]
            st = self.WStream(self, loads, depth=3, ring=slots)

            def stage2(bA, bB, rows, cols, w):
                a_sb, b_sb = AB.next(), AB.next()
                K.op(K.act, lambda: nc.scalar.copy(out=a_sb[:, 0:w], in_=bA[:, 0:w]), reads=[bA.R], writes=[a_sb.R])
                K.op(K.dve, lambda: nc.vector.tensor_copy(out=b_sb[:, 0:w], in_=bB[:, 0:w]), reads=[bB.R], writes=[b_sb.R])
                bY = self.P.next()
                K.mmg(bY[:, 0:w], [(C128[:, :], a_sb[:, 0:w]), (nS128[:, :], b_sb[:, 0:w])],
                      reads=[C128.R, nS128.R, a_sb.R, b_sb.R], writes=[bY.R])
                ys = ystg.next()
                K.op(K.act, lambda: nc.scalar.copy(out=ys[:, 0:w], in_=bY[:, 0:w]), reads=[bY.R], writes=[ys.R])
                K.dma(K.sp, self.SY[rows, cols], ys[:, 0:w], src=[ys.R], dst=self.R_SY, partial=True)

            li = 0
            for kb in range(4):
                for gp in range(2):
                    banks = {(tbl, g): self.P.next() for tbl in range(2) for g in (2 * gp, 2 * gp + 1)}
                    for tbl in range(2):
                        for q in range(4):
                            slot = st.get(li)
                            li += 1
                            for g in (2 * gp, 2 * gp + 1):
                                bk = banks[(tbl, g)]
                                K.mmg(bk[:, :], [(U[:, q * 8 + n, g * 128:(g + 1) * 128], slot[:, n, :]) for n in range(8)],
                                      reads=[U.R, slot.R], writes=[bk.R], start=(q == 0), stop=(q == 3))
                    for g in (2 * gp, 2 * gp + 1):
                        stage2(banks[(0, g)], banks[(1, g)], slice(512 + g * 128, 512 + (g + 1) * 128),
                               slice(kb * 512, (kb + 1) * 512), 512)
            regs = [U.R, C128.R, nS128.R] + [t.R for t in slots.tiles + AB.tiles + ystg.tiles]
            if first:
                Uc = Tile(K, "Uc", [128, 2, 512], BF16, es2)
                K.dma(K.sp, Uc[:], self.SUC.rearrange("(t p) c -> p t c", p=128), src=[self.R_SUC], dst=Uc.R)
                C256 = self.const_load("C256", [128, 2, 256], BF16, es2, src=I["C256"].rearrange("(n p) k -> p n k", p=128))
                S256 = self.const_load("S256", [128, 2, 256], BF16, es2, src=I["S256"].rearrange("(n p) k -> p n k", p=128))
                for g in range(4):
                    bA, bB = self.P.next(), self.P.next()
                    K.mmg(bA[:, 0:256], [(Uc[:, n, g * 128:(g + 1) * 128], C256[:, n, :]) for n in range(2)],
                          reads=[Uc.R, C256.R], writes=[bA.R])
                    K.mmg(bB[:, 0:256], [(Uc[:, n, g * 128:(g + 1) * 128], S256[:, n, :]) for n in range(2)],
                          reads=[Uc.R, S256.R], writes=[bB.R])
                    stage2(bA, bB, slice(512 + g * 128, 512 + (g + 1) * 128), slice(LT, LT + CT), 256)
                regs += [Uc.R, C256.R, S256.R]
            self.drain(regs)

    def phase_B34(self, l):
        K, nc, I = self.K, self.nc, self.I
        first = (l == 0)
        A = lambda fn, **kw: K.op(K.act, fn, **kw)
        V = lambda fn, **kw: K.op(K.dve, fn, **kw)
        G = lambda fn, **kw: K.op(K.pool, fn, **kw)
        WMAX = LT + 32
        with contextlib.ExitStack() as es2:
            scw = self.const_load("sc_w", [128, 4, 3], F32, es2, src=I["sc_w"][l])
            cfw = self.const_load("cf_w", [128, 4, 31], F32, es2, src=I["cf_w"][l])
            cfln = self.const_load("cf_ln", [128, 4, 2], F32, es2, src=I["cf_ln"][l])
            HLf = Tile(K, "HLf", [128, 20, 32], BF16, es2)
            HRf = Tile(K, "HRf", [128, 20, 32], BF16, es2)
            HL = Tile(K, "HL", [128, 20, 16], BF16, es2)
            HR = Tile(K, "HR", [128, 20, 16], BF16, es2)
            K.dma(K.sp, HLf[:, :, :].rearrange("p a w -> p (a w)"), self.EXHG[0:128, :], src=[self.R_EXHG], dst=HLf.R)
            K.dma(K.sp, HRf[:, :, :].rearrange("p a w -> p (a w)"), self.EXHG[128:256, :], src=[self.R_EXHG], dst=HRf.R)
            V(lambda: nc.vector.tensor_scalar(out=HL[:], in0=HLf[:, :, 16:32], scalar1=self.flags[:, 0:1], scalar2=None, op0=ALU.mult),
              reads=[HLf.R, self.flags.R], writes=[HL.R])
            V(lambda: nc.vector.tensor_scalar(out=HR[:], in0=HRf[:, :, 0:16], scalar1=self.flags[:, 1:2], scalar2=None, op0=ALU.mult),
              reads=[HRf.R, self.flags.R], writes=[HR.R])
            E = []
            for i in range(2):
                t = Tile(K, f"E{i}", [128, WMAX], BF16, es2)
                t.RH = Reg(f"E{i}h")
                E.append(t)
            Mf = Tile(K, "Mf", [128, WMAX], F32, es2)
            accD = Tile(K, "accD", [128, LT], F32, es2)
            accP = Tile(K, "accP", [128, LT], F32, es2)
            Bt = Tile(K, "Bt", [128, LT], BF16, es2)
            yo = Tile(K, "cyo", [128, LT], BF16, es2)
            CV = Tile(K, "CV", [128, 4, LT], F32, es2)
            SQ = Tile(K, "SQ", [128, 4, 512], F32, es2)
            mt = Tile(K, "mt", [128, 3, 512], F32, es2)
            tt = Ring([Tile(K, f"ctt{i}", [128, 512], F32, es2) for i in range(2)])
            og = Ring([Tile(K, f"cog{i}", [128, 512], BF16, es2) for i in range(2)])
            segs = [(0, LT, True)] + ([(LT, CT, False)] if first else [])

            def load_ext(e, kind, blk, tok0, W, halo):
                a = kind * 4 + blk
                if halo:
                    V(lambda: nc.vector.tensor_copy(out=e[:, 0:16], in_=HL[:, a, :]), reads=[HL.R], writes=[e.RH])
                    V(lambda: nc.vector.tensor_copy(out=e[:, 16 + W:32 + W], in_=HR[:, a, :]), reads=[HR.R], pwrites=[e.RH])
                else:
                    V(lambda: nc.vector.memset(e[:, 0:16], 0.0), writes=[e.RH])
                    V(lambda: nc.vector.memset(e[:, 16 + W:32 + W], 0.0), pwrites=[e.RH])
                r0 = kind * 512 + blk * 128
                K.dma(K.sp, e[:, 16:16 + W], self.SF[r0:r0 + 128, tok0:tok0 + W], src=[self.R_SF], dst=e.R)

            for (tok0, W, halo) in segs:
                for blk in range(4):
                    load_ext(E[0], 1, blk, tok0, W, halo)
                    load_ext(E[1], 2, blk, tok0, W, halo)
                    K.dma(K.sp, Bt[:, 0:W], self.SF[blk * 128:(blk + 1) * 128, tok0:tok0 + W], src=[self.R_SF], dst=Bt.R)
                    V(lambda: nc.vector.tensor_tensor(out=Mf[:, 0:W + 32], in0=E[0][:, 0:W + 32], in1=E[1][:, 0:W + 32], op=ALU.mult),
                      reads=[E[0].R, E[0].RH, E[1].R, E[1].RH], writes=[Mf.R])
                    V(lambda blk=blk: nc.vector.tensor_scalar(out=accD[:, 0:W], in0=Mf[:, 15:15 + W], scalar1=scw[:, blk, 0:1],
                                                               scalar2=None, op0=ALU.mult), reads=[Mf.R, scw.R], writes=[accD.R])
                    for j in (1, 2):
                        V(lambda blk=blk, j=j: nc.vector.scalar_tensor_tensor(out=accD[:, 0:W], in0=Mf[:, 15 + j:15 + j + W],
                                                                               scalar=scw[:, blk, j:j + 1], in1=accD[:, 0:W],
                                                                               op0=ALU.mult, op1=ALU.add),
                          reads=[Mf.R, scw.R, accD.R], writes=[accD.R])
                    V(lambda: nc.vector.tensor_tensor(out=yo[:, 0:W], in0=accD[:, 0:W], in1=Bt[:, 0:W], op=ALU.mult),
                      reads=[accD.R, Bt.R], writes=[yo.R])
                    K.dma(K.sp, self.SY[1024 + blk * 128:1024 + (blk + 1) * 128, tok0:tok0 + W], yo[:, 0:W],
                          src=[yo.R], dst=self.R_SY, partial=True)
                for blk in range(4):
                    load_ext(E[0], 3, blk, tok0, W, halo)
                    load_ext(E[1], 4, blk, tok0, W, halo)
                    A(lambda: nc.scalar.activation(out=Mf[:, 0:W + 32], in_=E[1][:, 0:W + 32], func=AF.Sigmoid),
                      reads=[E[1].R, E[1].RH], writes=[Mf.R])
                    V(lambda: nc.vector.tensor_tensor(out=Mf[:, 0:W + 32], in0=Mf[:, 0:W + 32], in1=E[0][:, 0:W + 32], op=ALU.mult),
                      reads=[Mf.R, E[0].R, E[0].RH], writes=[Mf.R])
                    V(lambda blk=blk: nc.vector.tensor_scalar(out=accD[:, 0:W], in0=Mf[:, 1:1 + W], scalar1=cfw[:, blk, 0:1],
                                                               scalar2=None, op0=ALU.mult), reads=[Mf.R, cfw.R], writes=[accD.R])
                    for j in range(1, 31):
                        V(lambda blk=blk, j=j: nc.vector.scalar_tensor_tensor(out=accD[:, 0:W], in0=Mf[:, 1 + j:1 + j + W],
                                                                               scalar=cfw[:, blk, j:j + 1], in1=accD[:, 0:W],
                                                                               op0=ALU.mult, op1=ALU.add),
                          reads=[Mf.R, cfw.R, accD.R], writes=[accD.R])
                    A(lambda blk=blk: nc.scalar.copy(out=CV[:, blk, 0:W], in_=accD[:, 0:W]), reads=[accD.R], pwrites=[CV.R])
                for tb in range(0, W, 512):
                    w = min(512, W - tb)
                    A(lambda tb=tb, w=w: nc.scalar.activation(out=SQ[:, :, 0:w], in_=CV[:, :, tb:tb + w], func=AF.Square),
                      reads=[CV.R], writes=[SQ.R])
                    b1, b2 = self.P.next(), self.P.next()
                    K.mmg(b1[:, 0:w], [(self.ones_f[:, :], CV[:, blk, tb:tb + w]) for blk in range(4)],
                          reads=[self.ones_f.R, CV.R], writes=[b1.R])
                    K.mmg(b2[:, 0:w], [(self.ones_f[:, :], SQ[:, blk, 0:w]) for blk in range(4)],
                          reads=[self.ones_f.R, SQ.R], writes=[b2.R])
                    A(lambda b1=b1, w=w: nc.scalar.mul(out=mt[:, 0, 0:w], in_=b1[:, 0:w], mul=1.0 / 512), reads=[b1.R], writes=[mt.R])
                    V(lambda w=w: nc.vector.tensor_tensor(out=mt[:, 1, 0:w], in0=mt[:, 0, 0:w], in1=mt[:, 0, 0:w], op=ALU.mult),
                      reads=[mt.R], writes=[mt.R])
                    V(lambda b2=b2, w=w: nc.vector.scalar_tensor_tensor(out=mt[:, 2, 0:w], in0=b2[:, 0:w], scalar=1.0 / 512,
                                                                         in1=mt[:, 1, 0:w], op0=ALU.mult, op1=ALU.subtract),
                      reads=[b2.R, mt.R], writes=[mt.R])
                    A(lambda w=w: nc.scalar.activation(out=mt[:, 1, 0:w], in_=mt[:, 2, 0:w], func=AF.Sqrt, bias=self.epsc[:, 0:1], scale=1.0),
                      reads=[mt.R, self.epsc.R], writes=[mt.R])
                    V(lambda w=w: nc.vector.reciprocal(out=mt[:, 2, 0:w], in_=mt[:, 1, 0:w]), reads=[mt.R], writes=[mt.R])
                    for blk in range(4):
                        t_ = tt.next()
                        V(lambda blk=blk, t_=t_, tb=tb, w=w: nc.vector.tensor_tensor(out=t_[:, 0:w], in0=CV[:, blk, tb:tb + w],
                                                                                      in1=mt[:, 0, 0:w], op=ALU.subtract),
                          reads=[CV.R, mt.R], writes=[t_.R])
                        G(lambda t_=t_, w=w: nc.gpsimd.tensor_tensor(out=t_[:, 0:w], in0=t_[:, 0:w], in1=mt[:, 2, 0:w], op=ALU.mult),
                          reads=[t_.R, mt.R], writes=[t_.R])
                        o = og.next()
                        A(lambda blk=blk, t_=t_, o=o, w=w: nc.scalar.activation(out=o[:, 0:w], in_=t_[:, 0:w], func=AF.Silu,
                                                                                 scale=cfln[:, blk, 0:1], bias=cfln[:, blk, 1:2]),
                          reads=[t_.R, cfln.R], writes=[o.R])
                        K.dma(K.sp, self.SY[1536 + blk * 128:1536 + (blk + 1) * 128, tok0 + tb:tok0 + tb + w], o[:, 0:w],
                              src=[o.R], dst=self.R_SY, partial=True)
            regs = [scw.R, cfw.R, cfln.R, HL.R, HR.R, HLf.R, HRf.R, Mf.R, accD.R, accP.R, Bt.R, yo.R, CV.R, SQ.R, mt.R]
            regs += [E[0].R, E[0].RH, E[1].R, E[1].RH] + [t.R for t in tt.tiles + og.tiles]
            self.drain(regs)

    def phase_C(self, l):
        K, nc, I = self.K, self.nc, self.I
        first, last = (l == 0), (l == DEPTH - 1)
        A = lambda fn, **kw: K.op(K.act, fn, **kw)
        V = lambda fn, **kw: K.op(K.dve, fn, **kw)
        G = lambda fn, **kw: K.op(K.pool, fn, **kw)
        NTOK = 256
        with contextlib.ExitStack() as es2:
            wsr = self.alloc_ws(es2)
            ROW = [Tile(K, f"row{v}", [128, D], F32, es2) for v in range(2)]
            YB = Tile(K, "YB", [128, 16, NTOK], BF16, es2)
            SGt = Ring([Tile(K, f"SGt{i}", [128, 4, NTOK], BF16, es2) for i in range(2)])
            WB = Ring([Tile(K, f"WB{i}", [128, 16, 128], BF16, es2) for i in range(2)])
            merged = Tile(K, "merged", [128, 16, NTOK], BF16, es2)
            tm = [Tile(K, f"tm{i}", [128, NTOK], F32, es2) for i in range(4)]
            MIX = Tile(K, "MIX", [128, 2, D], F32, es2)
            xm = [Tile(K, f"xm{i}", [128, D], F32, es2) for i in range(2)]
            ssq = Tile(K, "ssq", [128, 2, 8, 8], F32, es2)
            ssr = Tile(K, "ssrC", [128, 4], F32, es2)
            xn2 = Tile(K, "xn2", [128, D], BF16, es2)
            h2T = Tile(K, "h2T", [128, 16, NTOK], BF16, es2)
            aT = Tile(K, "aT", [128, 64, NTOK], BF16, es2)
            rl = Ring([Tile(K, f"rl{i}", [128, NTOK], BF16, es2) for i in range(2)])
            rtmp_regs = []

            def wloads():
                ls = []
                for cbk in range(4):
                    ls.append(lambda slot, cbk=cbk: self.wtile(slot, l, 26 + cbk, I["w_out"][l, :, cbk * 512:(cbk + 1) * 512]))
                for fb in range(16):
                    ls.append(lambda slot, fb=fb: self.wtile(slot, l, 30 + fb, I["w_ff1"][l, :, fb * 512:(fb + 1) * 512]))
                for cbk in range(4):
                    for q in range(4):
                        ls.append(lambda slot, cbk=cbk, q=q: self.wtile(
                            slot, l, 46 + cbk * 4 + q, I["w_ff2"][l, q * 2048:(q + 1) * 2048, cbk * 512:(cbk + 1) * 512]))
                return ls

            def rms_update(t, vrow):
                V(lambda: nc.vector.reduce_sum(out=ssq[:, t, 4, 0:1], in_=ssq[:, t, 0:4, 0], axis=AX.X), reads=[ssq.R], writes=[ssq.R])
                A(lambda: nc.scalar.activation(out=ssq[:, t, 5, 0:1], in_=ssq[:, t, 4, 0:1], func=AF.Sqrt, scale=1.0 / D,
                                               bias=self.epsc[:, 0:1]), reads=[ssq.R, self.epsc.R], writes=[ssq.R])
                V(lambda: nc.vector.reciprocal(out=ssq[:, t, 6, 0:1], in_=ssq[:, t, 5, 0:1]), reads=[ssq.R], writes=[ssq.R])
                V(lambda: nc.vector.scalar_tensor_tensor(out=MIX[:, t, :], in0=MIX[:, t, :], scalar=ssq[:, t, 6, 0:1],
                                                         in1=ROW[vrow][:, :], op0=ALU.mult, op1=ALU.mult),
                  reads=[MIX.R, ssq.R, ROW[vrow].R], writes=[MIX.R])
                V(lambda: nc.vector.tensor_tensor(out=xm[t][:, :], in0=xm[t][:, :], in1=MIX[:, t, :], op=ALU.add),
                  reads=[MIX.R, xm[t].R], writes=[xm[t].R])

            def evac_tok(bank, t, cbk):
                dstv = MIX[:, t, cbk * 512:(cbk + 1) * 512]
                V(lambda: nc.vector.tensor_copy(out=dstv, in_=bank[:, :]), reads=[bank.R], writes=[MIX.R])
                A(lambda: nc.scalar.activation(out=self.junk[:, 0:512], in_=dstv, func=AF.Square,
                                               accum_out=ssq[:, t, cbk, 0:1]), reads=[MIX.R], writes=[self.junk.R, ssq.R])

            def do_blocks(blocks, kind):
                loads = []
                for _ in blocks:
                    loads += wloads()
                st = self.WStream(self, loads)
                li = 0
                for t0 in blocks:
                    tok0 = t0 * 128
                    for t in range(2):
                        src, Rs = self.x_src(l, t0 + t)
                        K.dma(K.sp, xm[t][:, :], src, src=[Rs] if Rs else [], dst=xm[t].R)
                    K.dma(K.sp, YB[:, :, :], self.SY[:, tok0:tok0 + NTOK].rearrange("(a p) t -> p a t", p=128),
                          src=[self.R_SY], dst=YB.R)
                    if self.cstop == "C1":
                        return
                    for dblk in range(16):
                        wb = WB.next()
                        K.dma(K.pool, wb[:, :, :], I["w_branch"][l][:, :, dblk * 128:(dblk + 1) * 128].rearrange(
                            "n (wc p) c -> p (n wc) c", p=128), dst=wb.R)
                        sg = SGt.next()
                        K.dma(K.sp, sg[:, :, :], self.SG.rearrange("(n r) t -> r n t", n=4)[dblk * 128:(dblk + 1) * 128, :, tok0:tok0 + NTOK],
                              src=[self.R_SG], dst=sg.R)
                        for n in range(4):
                            bank = self.P.next()
                            K.mmg(bank[:, 0:NTOK], [(wb[:, n * 4 + wc, :], YB[:, n * 4 + wc, :]) for wc in range(4)],
                                  reads=[wb.R, YB.R], writes=[bank.R])
                            V(lambda bank=bank, n=n, sg=sg: nc.vector.tensor_tensor(out=tm[n][:, :], in0=bank[:, 0:NTOK], in1=sg[:, n, :],
                                                                                     op=ALU.mult), reads=[bank.R, sg.R], writes=[tm[n].R])
                        G(lambda: nc.gpsimd.tensor_tensor(out=tm[0][:, :], in0=tm[0][:, :], in1=tm[1][:, :], op=ALU.add),
                          reads=[tm[0].R, tm[1].R], writes=[tm[0].R])
                        G(lambda: nc.gpsimd.tensor_tensor(out=tm[2][:, :], in0=tm[2][:, :], in1=tm[3][:, :], op=ALU.add),
                          reads=[tm[2].R, tm[3].R], writes=[tm[2].R])
                        G(lambda dblk=dblk: nc.gpsimd.tensor_tensor(out=merged[:, dblk, :], in0=tm[0][:, :], in1=tm[2][:, :], op=ALU.add),
                          reads=[tm[0].R, tm[2].R], pwrites=[merged.R])
                    if self.cstop == "C2":
                        return
                    for cbk in range(4):
                        w = st.get(li)
                        li += 1
                        for t in range(2):
                            bank = self.P.next()
                            K.mmg(bank[:, :], [(merged[:, dc, t * 128:(t + 1) * 128], w[:, dc, :]) for dc in range(16)],
                                  reads=[merged.R, w.R], writes=[bank.R])
                            evac_tok(bank, t, cbk)
                    if self.cstop == "C2b":
                        return
                    for t in range(2):
                        rms_update(t, 0)
                    if self.cstop == "C3":
                        return
                    for t in range(2):
                        self.norm_mod_T(xm[t], 1, 3, kind, h2T, t * 128, xn2, ssr)
                    if self.cstop == "C4":
                        return
                    for fb in range(16):
                        w = st.get(li)
                        li += 1
                        for j in range(4):
                            bank = self.P.next()
                            K.mmg(bank[:, 0:NTOK], [(w[:, kc, j * 128:(j + 1) * 128], h2T[:, kc, :]) for kc in range(16)],
                                  reads=[w.R, h2T.R], writes=[bank.R])
                            r = rl.next()
                            A(lambda bank=bank, r=r: nc.scalar.activation(out=r[:, :], in_=bank[:, 0:NTOK], func=AF.Relu),
                              reads=[bank.R], writes=[r.R])
                            V(lambda r=r, fb=fb, j=j: nc.vector.tensor_tensor(out=aT[:, fb * 4 + j, :], in0=r[:, :], in1=r[:, :], op=ALU.mult),
                              reads=[r.R], pwrites=[aT.R])
                    if self.cstop == "C5":
                        return
                    for cbk in range(4):
                        banks = [self.P.next() for _ in range(2)]
                        for q in range(4):
                            w = st.get(li)
                            li += 1
                            for t in range(2):
                                K.mmg(banks[t][:, :], [(aT[:, q * 16 + fc, t * 128:(t + 1) * 128], w[:, fc, :]) for fc in range(16)],
                                      reads=[aT.R, w.R], writes=[banks[t].R], start=(q == 0), stop=(q == 3))
                        for t in range(2):
                            evac_tok(banks[t], t, cbk)
                    if self.cstop == "C6":
                        return
                    for t in range(2):
                        rms_update(t, 1)
                        gt = t0 + t
                        if last:
                            K.dma(K.sp, self.out[gt * 128:(gt + 1) * 128, :], xm[t][:, :], src=[xm[t].R], dst=self.R_out, partial=True)
                        else:
                            K.dma(K.sp, self.XS[gt * 128:(gt + 1) * 128, :], xm[t][:, :], src=[xm[t].R], dst=self.R_XS, partial=True)

            if first:
                rtmp_regs += self.phase_Prow(l, 1, ROW, es2)
                do_blocks([NLT], 1)
            if self.cstop is None or self.cstop.startswith("L"):
                rtmp_regs += self.phase_Prow(l, 0, ROW, es2)
                do_blocks([2 * i for i in range(8 if self.cstop is None else int(self.cstop[1:]))], 0)
            regs = wsr + rtmp_regs + [t.R for t in ROW + [YB, merged, MIX, ssq, ssr, xn2, h2T, aT] + tm + xm + SGt.tiles + WB.tiles + rl.tiles]
            self.drain(regs)


_CONST_CACHE = {}


def _prep_inputs(x, c, ctx, c_ctx, w_ada, b_ada, norm_g, w_in, ret_decay, sc_conv, cf_conv, cf_ln,
                 w_branch, w_out, w_ff1, w_ff2):
    f = lambda a: np.ascontiguousarray(np.asarray(a), dtype=np.float32)
    x, c, ctx, c_ctx = f(x), f(c), f(ctx), f(c_ctx)
    shared = {
        "w_ada": f(w_ada), "w_in": f(w_in), "w_branch": f(w_branch), "w_out": f(w_out),
        "w_ff1": f(w_ff1), "w_ff2": f(w_ff2),
        "b_col": np.ascontiguousarray(f(b_ada).reshape(DEPTH, 96, 128).transpose(0, 2, 1)),
        "b_row": f(b_ada).reshape(DEPTH, 1, 6 * D),
        "ng_col": np.ascontiguousarray(f(norm_g).reshape(DEPTH, 4, 16, 128).transpose(0, 3, 1, 2)),
        "ng_row": f(norm_g),
        "ret_decay": np.ascontiguousarray(np.broadcast_to(f(ret_decay).reshape(DEPTH, 1, 8), (DEPTH, 128, 8))),
        "sc_w": np.ascontiguousarray(f(sc_conv).reshape(DEPTH, 3, 4, 128).transpose(0, 3, 2, 1)),
        "cf_w": np.ascontiguousarray(f(cf_conv).reshape(DEPTH, 31, 4, 128).transpose(0, 3, 2, 1)),
        "cf_ln": np.ascontiguousarray(f(cf_ln).reshape(DEPTH, 2, 4, 128).transpose(0, 3, 2, 1)),
    }
    for hf in range(2):
        if hf not in _CONST_CACHE:
            _CONST_CACHE[hf] = _host_consts(hf)
    in_maps = []
    for core in range(NCORES):
        b, hf = core // 2, core % 2
        m = dict(shared)
        m.update(_CONST_CACHE[hf])
        m["x_in"] = np.ascontiguousarray(x[b, hf * LT:(hf + 1) * LT, :])
        m["ctx_in"] = np.ascontiguousarray(ctx[b])
        cv = np.stack([c[b], c_ctx], -1).reshape(16, 128, 2).transpose(1, 0, 2)
        m["cvec"] = np.ascontiguousarray(cv)
        in_maps.append(m)
    return in_maps


_NC_CACHE = {}


def kernel(**inputs):
    in_maps = _prep_inputs(**inputs)
    if "nc" not in _NC_CACHE:
        _NC_CACHE["nc"] = Prog().build()
    res = run_bass_kernel_spmd(_NC_CACHE["nc"], in_maps, core_ids=list(range(NCORES)))
    out = np.empty((4, 2 * LT, D), np.float32)
    for core in range(NCORES):
        b, hf = core // 2, core % 2
        out[b, hf * LT:(hf + 1) * LT, :] = res.results[core]["out"]
    return out
```
